# Optimizing a Trainium2 kernel written in Bass

```python
import math
import jax, jax.numpy as jnp
from jax import lax
import numpy as np

D_MODEL = 1024
BATCH = 16
SEQ = 256
DEPTH = 2
DEC_BATCH = 4
DEC_SEQ = 1024
PAST_LEN = 256

GRID_W = 64
EPS = 1e-6
MLA_HEADS = 8
MLA_NOPE = 64
MLA_ROPE = 32
MLA_QK = MLA_NOPE + MLA_ROPE
MLA_V = 64
MLA_Q_RANK = 384
MLA_KV_RANK = 256
MLA_WIDTH = MLA_HEADS * MLA_V
ROPE_THETA = 10000.0
ATTN_BLOCK = 128
GDN_HEADS = 4
GDN_DK = 128
GDN_DV = 128
GDN_KW = GDN_HEADS * GDN_DK
GDN_VW = GDN_HEADS * GDN_DV
GDN_CHUNK = 64
CONV_W = 5
CM_GROUPS = 4
CM_CHUNK = 128
CM_WIDTH = 512
CM_GW = CM_WIDTH // CM_GROUPS
N_BRANCH = 3
BRANCH_W = 512
SPLIT_SIZES = (MLA_Q_RANK, MLA_KV_RANK, MLA_ROPE, MLA_WIDTH,
               2 * GDN_KW + GDN_VW, 2 * GDN_HEADS, 2 * GDN_HEADS, GDN_VW,
               CM_WIDTH, CM_WIDTH, CM_WIDTH, N_BRANCH * D_MODEL)
D_IN = sum(SPLIT_SIZES)

kernel_name = "hybrid_mla_gdn_chunkmlp_prefix_diffusion_step"


def _rms(x, g):
    xf = x.astype(jnp.float32)
    y = xf * lax.rsqrt(jnp.mean(xf * xf, axis=-1, keepdims=True) + EPS)
    return (y * g.astype(jnp.float32)).astype(x.dtype)


def _l2n(x):
    xf = x.astype(jnp.float32)
    return xf * lax.rsqrt(jnp.sum(xf * xf, axis=-1, keepdims=True) + EPS)


def _axial_rope(x, rows):
    row = jnp.repeat(jnp.arange(rows, dtype=jnp.float32), GRID_W)
    col = jnp.tile(jnp.arange(GRID_W, dtype=jnp.float32), rows)
    half = MLA_ROPE // 2
    nf = half // 2
    inv = ROPE_THETA ** (-jnp.arange(nf, dtype=jnp.float32) / nf)
    xf = x.astype(jnp.float32)

    def rot(xa, pos):
        ang = pos[:, None] * inv[None, :]
        cos = jnp.cos(ang)[:, None, :]
        sin = jnp.sin(ang)[:, None, :]
        x1, x2 = xa[..., :nf], xa[..., nf:]
        return jnp.concatenate([x1 * cos - x2 * sin, x1 * sin + x2 * cos], axis=-1)

    return jnp.concatenate([rot(xf[..., :half], row), rot(xf[..., half:], col)], axis=-1).astype(x.dtype)


def _attend(q, k, v):
    B, Tq, H, d = q.shape
    nb = Tq // ATTN_BLOCK
    qb = jnp.moveaxis(q.reshape(B, nb, ATTN_BLOCK, H, d), 1, 0)
    scale = 1.0 / math.sqrt(d)

    def one(qi):
        s = jnp.einsum('bqhd,bkhd->bhqk', qi, k).astype(jnp.float32) * scale
        pr = jax.nn.softmax(s, axis=-1)
        return jnp.einsum('bhqk,bkhd->bqhd', pr.astype(v.dtype), v)

    o = lax.map(one, qb)
    return jnp.moveaxis(o, 0, 1).reshape(B, Tq, H, v.shape[-1])


def _mla_kv(ckv_n, krope, p):
    B, T, _ = ckv_n.shape
    kv = (ckv_n @ p['w_ukv']).reshape(B, T, MLA_HEADS, MLA_NOPE + MLA_V)
    kr = jnp.broadcast_to(krope[:, :, None, :], (B, T, MLA_HEADS, MLA_ROPE))
    k = _rms(jnp.concatenate([kv[..., :MLA_NOPE], kr], axis=-1), p['k_norm'])
    return k, kv[..., MLA_NOPE:]


def _dwconv(x, w):
    return lax.conv_general_dilated(x, w[:, None, :], window_strides=(1,),
                                    padding=[(CONV_W // 2, CONV_W // 2)],
                                    dimension_numbers=('NWC', 'WIO', 'NWC'),
                                    feature_group_count=x.shape[-1])


def _gdn_chunk(q, k, v, g, beta, s0):
    B, T, H, dk = q.shape
    C = GDN_CHUNK
    N = T // C

    def blk(a):
        a = a.reshape((B, N, C, H) + a.shape[3:])
        return jnp.moveaxis(jnp.moveaxis(a, 1, 0), 3, 2)

    qc, kc, vc = blk(q * (dk ** -0.5)), blk(k), blk(v)
    gc = jnp.cumsum(blk(g), axis=-1)
    bc = blk(beta)
    kb = kc * bc[..., None]
    vb = vc * bc[..., None]
    incl = jnp.tril(jnp.ones((C, C), bool))
    strict = jnp.tril(jnp.ones((C, C), bool), -1)
    diff = gc[..., :, None] - gc[..., None, :]
    decay = jnp.where(incl, jnp.exp(jnp.where(incl, diff, 0.0)), 0.0)
    A = jnp.where(strict, jnp.einsum('nbhid,nbhjd->nbhij', kb, kc) * decay, 0.0)
    eye = jnp.eye(C, dtype=A.dtype)
    Tinv = lax.linalg.triangular_solve(A + eye, jnp.broadcast_to(eye, A.shape),
                                       left_side=True, lower=True, unit_diagonal=True)
    u = Tinv @ vb
    w = Tinv @ (kb * jnp.exp(gc)[..., None])

    def step(S, xs):
        qi, ki, ui, wi, gi, di = xs
        v_new = ui - wi @ S
        att = jnp.einsum('bhid,bhjd->bhij', qi, ki) * di
        o = (qi * jnp.exp(gi)[..., None]) @ S + att @ v_new
        gl = gi[..., -1:]
        S = S * jnp.exp(gl)[..., None] + jnp.einsum('bhcd,bhce->bhde', ki * jnp.exp(gl - gi)[..., None], v_new)
        return S, o

    S, o = lax.scan(step, s0, (qc, kc, u, w, gc, decay))
    o = jnp.moveaxis(jnp.moveaxis(o, 2, 3), 0, 1).reshape(B, T, H, v.shape[-1])
    return o, S


def _chunk_mlp(u, v, p):
    B, T, _ = u.shape
    u = jax.nn.gelu(u)
    vf = jax.nn.gelu(v).astype(jnp.float32)
    mu = jnp.mean(vf, axis=-1, keepdims=True)
    var = jnp.mean(jnp.square(vf - mu), axis=-1, keepdims=True)
    vn = ((vf - mu) * lax.rsqrt(var + EPS) * p['cm_ln_g'] + p['cm_ln_b']).astype(u.dtype)
    vr = vn.reshape(B, T // CM_CHUNK, CM_CHUNK, CM_GROUPS, CM_GW)
    sv = jnp.einsum('gpq,bnqgc->bnpgc', p['w_s'], vr) + jnp.swapaxes(p['b_s'], 0, 1)[:, :, None]
    return u * sv.reshape(B, T, CM_WIDTH)


def _layer(x, mod, p, cache=None):
    B, T, _ = x.shape
    f32 = jnp.float32
    shift, scale, gate = jnp.split(mod, 3, axis=-1)
    h = _rms(x, p['norm_g']) * (1 + scale) + shift
    proj = h @ p['w_in']
    split_pts = np.cumsum(SPLIT_SIZES)[:-1].tolist()
    (cq, ckv, krope, z_a, qkv_b, ga, gb, z_b, cu, cv, z_c, gl) = jnp.split(proj, split_pts, axis=-1)

    q = (_rms(cq, p['q_a_norm']) @ p['w_uq']).reshape(B, T, MLA_HEADS, MLA_QK)
    q = _rms(q, p['q_norm'])
    ckv_n = _rms(ckv, p['kv_a_norm'])
    k, v = _mla_kv(ckv_n, krope, p)
    if cache is None:
        o_a = _attend(q, k, v)
        s0 = jnp.zeros((B, 2, GDN_HEADS, GDN_DK, GDN_DV), f32)
    else:
        c_ckv, c_krope, s0 = cache
        rows = T // GRID_W
        q = jnp.concatenate([q[..., :MLA_NOPE], _axial_rope(q[..., MLA_NOPE:], rows)], axis=-1)
        k = jnp.concatenate([k[..., :MLA_NOPE], _axial_rope(k[..., MLA_NOPE:], rows)], axis=-1)
        kc, vc = _mla_kv(c_ckv, c_krope, p)
        o_a = _attend(q, jnp.concatenate([kc, k], axis=1), jnp.concatenate([vc, v], axis=1))
    o_a = o_a.reshape(B, T, MLA_WIDTH)

    qkv = jax.nn.silu(_dwconv(qkv_b, p['conv_w']))
    gq, gk, gv = jnp.split(qkv, [GDN_KW, 2 * GDN_KW], axis=-1)
    gq = _l2n(gq.reshape(B, T, GDN_HEADS, GDN_DK))
    gk = _l2n(gk.reshape(B, T, GDN_HEADS, GDN_DK))
    gv = gv.reshape(B, T, GDN_HEADS, GDN_DV).astype(f32)
    beta = jax.nn.sigmoid(gb.astype(f32)).reshape(B, T, 2, GDN_HEADS)
    g = -jnp.exp(p['a_log'].astype(f32)) * jax.nn.softplus(
        ga.astype(f32).reshape(B, T, 2, GDN_HEADS) + p['dt_bias'].astype(f32))
    s0 = s0.astype(f32)
    o_f, s_f = _gdn_chunk(gq, gk, gv, g[:, :, 0], beta[:, :, 0], s0[:, 0])
    fl = lambda t: jnp.flip(t, axis=1)
    o_bw, s_b = _gdn_chunk(fl(gq), fl(gk), fl(gv), fl(g[:, :, 1]), fl(beta[:, :, 1]), s0[:, 1])
    o_b = _rms(o_f + fl(o_bw), p['gdn_onorm']).astype(x.dtype).reshape(B, T, GDN_VW)

    o_c = _chunk_mlp(cu, cv, p)

    br = jnp.stack([o_a * jax.nn.silu(z_a), o_b * jax.nn.silu(z_b), o_c * jax.nn.silu(z_c)], axis=2)
    yb = jnp.einsum('btnw,nwd->btnd', br, p['w_branch'])
    gates = jax.nn.sigmoid(gl.reshape(B, T, N_BRANCH, D_MODEL))
    y = jnp.sum(gates * yb, axis=2) @ p['w_o']
    x = x + gate * y
    if cache is None:
        return x, (ckv_n, krope, jnp.stack([s_f, s_b], axis=1))
    return x, None


def setup_inputs(seed: int = 0) -> dict:
    key = jax.random.key(seed)
    ks = jax.random.split(key, 32)
    f32 = jnp.float32
    L = DEPTH

    def nrm(k, shape, s):
        return s * jax.random.normal(k, shape, f32)

    dt = jnp.exp(jax.random.uniform(ks[20], (L, 2, GDN_HEADS), f32, math.log(1e-3), math.log(1e-1)))
    return {
        'x_prompt': nrm(ks[0], (BATCH, SEQ, D_MODEL), 1.0),
        'x_sample': nrm(ks[1], (DEC_BATCH, DEC_SEQ, D_MODEL), 1.0),
        'cache_ckv': nrm(ks[2], (DEC_BATCH, DEPTH, PAST_LEN, MLA_KV_RANK), 1.0),
        'cache_krope': nrm(ks[3], (DEC_BATCH, DEPTH, PAST_LEN, MLA_ROPE), 1.0),
        'state_gdn': nrm(ks[4], (DEC_BATCH, DEPTH, 2, GDN_HEADS, GDN_DK, GDN_DV), 0.3),
        'c': nrm(ks[5], (DEC_BATCH, D_MODEL), 1.0),
        'c_ctx': nrm(ks[6], (D_MODEL,), 1.0),
        'norm_g': 1.0 + nrm(ks[7], (L, D_MODEL), 0.02),
        'w_mod': nrm(ks[8], (L, D_MODEL, 3 * D_MODEL), 0.5 * D_MODEL ** -0.5),
        'b_mod': nrm(ks[9], (L, 3 * D_MODEL), 0.02),
        'w_in': nrm(ks[10], (L, D_MODEL, D_IN), D_MODEL ** -0.5),
        'q_a_norm': 1.0 + nrm(ks[11], (L, MLA_Q_RANK), 0.02),
        'w_uq': nrm(ks[12], (L, MLA_Q_RANK, MLA_HEADS * MLA_QK), MLA_Q_RANK ** -0.5),
        'kv_a_norm': 1.0 + nrm(ks[13], (L, MLA_KV_RANK), 0.02),
        'w_ukv': nrm(ks[14], (L, MLA_KV_RANK, MLA_HEADS * (MLA_NOPE + MLA_V)), MLA_KV_RANK ** -0.5),
        'q_norm': 1.0 + nrm(ks[15], (L, MLA_QK), 0.02),
        'k_norm': 1.0 + nrm(ks[16], (L, MLA_QK), 0.02),
        'conv_w': nrm(ks[17], (L, CONV_W, 2 * GDN_KW + GDN_VW), CONV_W ** -0.5),
        'a_log': jnp.log(jax.random.uniform(ks[18], (L, 2, GDN_HEADS), f32, 1.0, 16.0)),
        'dt_bias': dt + jnp.log(-jnp.expm1(-dt)),
        'gdn_onorm': 1.0 + nrm(ks[19], (L, GDN_DV), 0.02),
        'cm_ln_g': 1.0 + nrm(ks[21], (L, CM_WIDTH), 0.02),
        'cm_ln_b': nrm(ks[22], (L, CM_WIDTH), 0.02),
        'w_s': nrm(ks[23], (L, CM_GROUPS, CM_CHUNK, CM_CHUNK), CM_CHUNK ** -0.5),
        'b_s': 1.0 + nrm(ks[24], (L, CM_GROUPS, CM_CHUNK), 0.02),
        'w_branch': nrm(ks[25], (L, N_BRANCH, BRANCH_W, D_MODEL), BRANCH_W ** -0.5),
        'w_o': nrm(ks[26], (L, D_MODEL, D_MODEL), D_MODEL ** -0.5),
    }


def reference(x_prompt, x_sample, cache_ckv, cache_krope, state_gdn, c, c_ctx, norm_g, w_mod, b_mod,
              w_in, q_a_norm, w_uq, kv_a_norm, w_ukv, q_norm, k_norm, conv_w, a_log, dt_bias,
              gdn_onorm, cm_ln_g, cm_ln_b, w_s, b_s, w_branch, w_o):
    yp = x_prompt
    ys = x_sample
    ckvs, kropes, states = [], [], []
    for l in range(DEPTH):
        p = {'norm_g': norm_g[l], 'w_in': w_in[l], 'q_a_norm': q_a_norm[l], 'w_uq': w_uq[l],
             'kv_a_norm': kv_a_norm[l], 'w_ukv': w_ukv[l], 'q_norm': q_norm[l], 'k_norm': k_norm[l],
             'conv_w': conv_w[l], 'a_log': a_log[l], 'dt_bias': dt_bias[l], 'gdn_onorm': gdn_onorm[l],
             'cm_ln_g': cm_ln_g[l], 'cm_ln_b': cm_ln_b[l], 'w_s': w_s[l], 'b_s': b_s[l],
             'w_branch': w_branch[l], 'w_o': w_o[l]}
        mod_ctx = (jax.nn.silu(c_ctx) @ w_mod[l] + b_mod[l])[None, None, :]
        mod_lat = (jax.nn.silu(c) @ w_mod[l] + b_mod[l])[:, None, :]
        yp, (ckv_l, kr_l, s_l) = _layer(yp, mod_ctx, p)
        ys, _ = _layer(ys, mod_lat, p, (cache_ckv[:, l], cache_krope[:, l], state_gdn[:, l]))
        ckvs.append(ckv_l)
        kropes.append(kr_l)
        states.append(s_l)
    new_ckv = jnp.stack(ckvs, axis=1)
    new_krope = jnp.stack(kropes, axis=1)
    new_state = jnp.stack(states, axis=1)
    return (yp, ys, new_ckv, new_krope, new_state)
```

```python
import numpy as np
import concourse.bass as bass
import concourse.mybir as mybir

F32 = mybir.dt.float32
BF16 = mybir.dt.bfloat16
ALU = mybir.AluOpType
AF = mybir.ActivationFunctionType
AX = mybir.AxisListType

_DT_SIZE = {F32: 4, BF16: 2}


def _dsize(dt):
    if dt in _DT_SIZE:
        return _DT_SIZE[dt]
    return mybir.dt.size(dt)


def ap_box(ap):
    t = ap.tensor
    name = t.name
    dims = ap.ap
    esz = _dsize(ap.dtype)
    off = int(ap.offset)
    shape = list(t.shape)
    space = str(ap.space)
    if 'DRAM' in space.upper() or 'HBM' in space.upper() or 'dram' in space.lower():
        lo = off
        hi = off
        for (st, cnt) in dims:
            if st >= 0:
                hi += st * (cnt - 1)
            else:
                lo += st * (cnt - 1)
        return (name, 0, 1, lo * esz, (hi + 1) * esz)
    tsz = _dsize(t.dtype)
    row = 1
    for s in shape[1:]:
        row *= s
    row_e = row * tsz // esz
    pstep, pcnt = dims[0]
    p0 = off // row_e
    fo = off % row_e
    if pstep == 0:
        np_ = 1
    else:
        np_ = pcnt * (pstep // row_e) if pstep >= row_e else 1
        np_ = (pcnt - 1) * (pstep // row_e) + 1
    lo = fo
    hi = fo
    for (st, cnt) in dims[1:]:
        if st >= 0:
            hi += st * (cnt - 1)
        else:
            lo += st * (cnt - 1)
    return (name, p0, p0 + np_, lo * esz, (hi + 1) * esz)


ENGS = ['pe', 'act', 'dve', 'pool', 'sp']
SAME_DIST = 10


class Prog:
    def __init__(self, nc, n_dma_sems=20):
        self.nc = nc
        self.ops = {e: [] for e in ENGS}
        self.clock = {e: {} for e in ENGS}
        self.snap = {}
        self.recs = {}
        self.n_dma = n_dma_sems
        self.dma_val = [0] * n_dma_sems
        self.dma_rr = 0
        self.dma_rr_sw = 0
        self.n_sw = 8
        self.waited = {e: set() for e in ENGS}
        self.out_events = []

    def _known(self, eng, ev):
        c = self.clock[eng]
        if ev[0] == 'e':
            return c.get(ev[1], -1) >= ev[2]
        return c.get(('d', ev[1]), 0) >= ev[2]

    def _learn(self, eng, ev):
        c = self.clock[eng]
        s = self.snap.get(ev)
        if s:
            for k, v in s.items():
                if c.get(k, -1) < v:
                    c[k] = v
        if ev[0] == 'e':
            if c.get(ev[1], -1) < ev[2]:
                c[ev[1]] = ev[2]
        else:
            k = ('d', ev[1])
            if c.get(k, 0) < ev[2]:
                c[k] = ev[2]

    def add(self, eng, fn, reads=(), writes=(), dma=False, extra_deps=()):
        idx = len(self.ops[eng])
        deps = set(extra_deps)

        def bx(a):
            b = ap_box(a)
            if b[0].startswith('ps'):
                return (b[0], 0, 128, 0, 1 << 20)
            return b
        rboxes = [bx(a) for a in reads]
        wboxes = [bx(a) for a in writes]
        for boxes, isw in ((rboxes, False), (wboxes, True)):
            for (name, p0, p1, b0, b1) in boxes:
                ps = name.startswith('ps')
                for r in self.recs.get(name, ()):
                    if r[0] < p1 and p0 < r[1] and r[2] < b1 and b0 < r[3]:
                        if isw or r[5] or (ps and r[6] != eng):
                            deps.add(r[4])
        if dma:
            if eng == 'pool':
                semi = self.dma_rr_sw
                self.dma_rr_sw = (self.dma_rr_sw + 1) % self.n_sw
            else:
                semi = self.n_sw + self.dma_rr
                self.dma_rr = (self.dma_rr + 1) % (self.n_dma - self.n_sw)
            prev = self.dma_val[semi]
            if prev > 0:
                deps.add(('d', semi, prev))
            self.dma_val[semi] = prev + 16
            ev = ('d', semi, prev + 16)
        else:
            ev = ('e', eng, idx)
        waits = []
        for d in sorted(deps, key=lambda d: -d[2]):
            if d[0] == 'e' and d[1] == eng:
                if eng == 'pe' and not dma:
                    continue
                if self._known(eng, d):
                    continue
                if (idx - d[2]) > SAME_DIST and not dma:
                    continue
                waits.append(d)
                self.waited[eng].add(d[2])
                self._learn(eng, d)
                continue
            if self._known(eng, d):
                continue
            waits.append(d)
            if d[0] == 'e':
                self.waited[d[1]].add(d[2])
            self._learn(eng, d)
        self.snap[ev] = dict(self.clock[eng])
        self.ops[eng].append(dict(fn=fn, waits=waits, dma=dma, ev=ev))
        acc_eng = ('dma:' + eng) if dma else eng
        for (name, p0, p1, b0, b1) in wboxes:
            lst = self.recs.setdefault(name, [])
            if name.startswith('ps'):
                lst[:] = []
            else:
                lst[:] = [r for r in lst if not (p0 <= r[0] and r[1] <= p1 and b0 <= r[2] and r[3] <= b1)]
            lst.append([p0, p1, b0, b1, ev, True, acc_eng])
        for (name, p0, p1, b0, b1) in rboxes:
            lst = self.recs.setdefault(name, [])
            if name.startswith('ps'):
                lst[:] = [r for r in lst if r[6] == eng and r[5]]
            elif not dma:
                lst[:] = [r for r in lst if not ((not r[5]) and r[4][0] == 'e' and r[4][1] == eng
                                                 and p0 <= r[0] and r[1] <= p1 and b0 <= r[2] and r[3] <= b1)]
            lst.append([p0, p1, b0, b1, ev, False, acc_eng])
        return ev

    def barrier(self):
        evs = []
        for e in ENGS:
            n = len(self.ops[e])
            if n:
                for i in range(n - 1, -1, -1):
                    if (not self.ops[e][i]['dma']) and self.ops[e][i]['fn'] is not None:
                        evs.append(('e', e, i))
                        break
        for i, v in enumerate(self.dma_val):
            if v > 0:
                evs.append(('d', i, v))
        for e in ENGS:
            self.add(e, None, extra_deps=[x for x in evs if not (x[0] == 'e' and x[1] == e)])
        self.recs = {}

    def final_wait(self, eng='sp'):
        evs = [('d', i, v) for i, v in enumerate(self.dma_val) if v > 0]
        self.add(eng, None, extra_deps=evs)

    def emit(self):
        nc = self.nc
        import contextlib
        with contextlib.ExitStack() as st:
            esem = {e: st.enter_context(nc.semaphore('s_' + e)) for e in ENGS}
            dsem = [st.enter_context(nc.semaphore('d%d' % i)) for i in range(self.n_dma)]
            rank = {}
            for e in ENGS:
                c = 0
                for i, op in enumerate(self.ops[e]):
                    if (not op['dma']) and i in self.waited[e]:
                        c += 1
                        rank[(e, i)] = c
            block = st.enter_context(nc.Block())

            def run(e, engobj):
                for i, op in enumerate(self.ops[e]):
                    for w in op['waits']:
                        if w[0] == 'e':
                            engobj.wait_ge(esem[w[1]], rank[(w[1], w[2])])
                        else:
                            engobj.wait_ge(dsem[w[1]], w[2])
                    if op['fn'] is None:
                        assert (e, i) not in rank
                        continue
                    ins = op['fn'](engobj)
                    if op['dma']:
                        ins.then_inc(dsem[op['ev'][1]], 16)
                    elif (e, i) in rank:
                        ins.then_inc(esem[e], 1)

            @block.tensor
            def _(eng):
                run('pe', eng)

            @block.scalar
            def _(eng):
                run('act', eng)

            @block.vector
            def _(eng):
                run('dve', eng)

            @block.gpsimd
            def _(eng):
                run('pool', eng)

            @block.sync
            def _(eng):
                run('sp', eng)


from concourse.bass_utils import run_bass_kernel_spmd
import contextlib, math

D = 1024
NROW = 2000
NCOL = 99
R_KVN, R_QN, R_KN, R_GON, R_LNG, R_LNB, R_ALOG, R_DTB, R_QAN = 0, 256, 352, 448, 576, 1088, 1600, 1608, 1616
C_NG, C_BMOD, C_CONV, C_BS = 0, 8, 32, 95
EPS = 1e-6
STAGE_A, STAGE_B, STAGE_C = 1, 1, 1
B2_STAGGER = 16
O_CQ, O_CKV, O_KR, O_ZA, O_QKV, O_GA, O_GB, O_ZB, O_CU, O_CV, O_ZC, O_GL = 0, 384, 640, 672, 1184, 2720, 2728, 2736, 3248, 3760, 4272, 4784


def isap(v):
    return isinstance(v, bass.AP)


def build_program(stage=99, dbg=False):
    nc = bass.Bass("TRN2", target_bir_lowering=False)

    def din(name, shape):
        return nc.dram_tensor(name, shape, F32, kind="ExternalInput").ap()

    def dout(name, shape):
        return nc.dram_tensor(name, shape, F32, kind="ExternalOutput").ap()

    xin = din("xin", [1024, 1024])
    ccol = din("ccol", [128, 8])
    cck = din("cck", [2, 256, 256])
    ckr = din("ckr", [2, 256, 32])
    s0d = din("s0", [2, 4, 2, 4, 128, 128])
    keepd = din("keep", [128, 1])
    ropec = din("ropec", [128, 10, 32])
    ropes = din("ropes", [128, 10, 32])
    qseg = din("qseg", [4, 1024])
    kpen = din("kpen", [4, 1280])
    rowpd = din("rowp", [2, NROW])
    colpd = din("colp", [128, 2 * NCOL])
    bgate = din("bgate", [2, 1024])
    w_mod = din("w_mod", [2, 1024, 3072])
    w_in = din("w_in", [2, 1024, 7856])
    w_uq = din("w_uq", [2, 384, 768])
    w_ukv = din("w_ukv", [2, 256, 1024])
    w_sT = din("w_sT", [2, 128, 4, 128])
    w_br = din("w_br", [2, 3, 512, 1024])
    w_o = din("w_o", [2, 1024, 1024])
    masksd = din("masks", [128, 4, 128])
    identd = din("ident", [128, 128])
    y_o = dout("y", [1024, 1024])
    nckv_o = dout("nckv", [2, 1024, 256])
    nkr_o = dout("nkr", [2, 1024, 32])
    nst_o = dout("nst", [2, 4, 2, 4, 128, 128])

    P = Prog(nc, n_dma_sems=24)

    def I(eng, meth, **kw):
        wr = [kw[k] for k in ('out', 'accum_out') if isap(kw.get(k))]
        rd = [v for k, v in kw.items() if k not in ('out', 'accum_out') and isap(v)]
        return P.add(eng, lambda e: getattr(e, meth)(**kw), rd, wr)

    def DMA(eng, out, in_):
        return P.add(eng, lambda e: e.dma_start(out=out, in_=in_), [in_], [out], dma=True)

    def MM(out, lhsT, rhs, start=True, stop=True):
        return I('pe', 'matmul', out=out, lhsT=lhsT, rhs=rhs, start=start, stop=stop)

    def TR(out, in_, ident):
        return I('pe', 'transpose', out=out, in_=in_, identity=ident)

    def ACT(out, in_, func, **kw):
        return I('act', 'activation', out=out, in_=in_, func=func, **kw)

    def TT(eng, out, in0, in1, op):
        return I(eng, 'tensor_tensor', out=out, in0=in0, in1=in1, op=op)

    def TS(eng, out, in0, s1, op0, s2=None, op1=None):
        if op1 is None:
            return I(eng, 'tensor_scalar', out=out, in0=in0, scalar1=s1, scalar2=None, op0=op0)
        return I(eng, 'tensor_scalar', out=out, in0=in0, scalar1=s1, scalar2=s2, op0=op0, op1=op1)

    def STT(eng, out, in0, scalar, in1, op0, op1):
        return I(eng, 'scalar_tensor_tensor', out=out, in0=in0, scalar=scalar, in1=in1, op0=op0, op1=op1)

    def CP(eng, out, in_):
        return I(eng, 'tensor_copy', out=out, in_=in_)

    def RSUM(eng, out, in_):
        return I(eng, 'tensor_reduce', out=out, in_=in_, axis=AX.X, op=ALU.add)

    def rstd_from_ssq(out, ssq, n):
        ACT(out, ssq, AF.Sqrt, bias=EPS, scale=1.0 / n)
        I('dve', 'reciprocal', out=out, in_=out)

    with contextlib.ExitStack() as st:
        ucnt = [0]
        pers_bytes = [0]
        arena_box = {}
        arena_top = [0]

        def sb(name, shape, dt=F32, stack=None):
            esz = 4 if dt == F32 else 2
            n = 1
            for d_ in shape[1:]:
                n *= d_
            if stack is None:
                pers_bytes[0] += (n * esz + 31) // 32 * 32
                return st.enter_context(nc.sbuf_tensor(name, shape, dt))
            if 'a' not in arena_box:
                ab = (212800 - pers_bytes[0] - 256) // 64 * 64
                arena_box['a'] = st.enter_context(nc.sbuf_tensor("arena", [128, ab // 2], BF16))
                arena_box['n'] = ab
            nbytes = n * esz
            nb_al = (nbytes + 31) // 32 * 32
            off = arena_top[0]
            assert off + nb_al <= arena_box['n'], ("arena overflow", name, off, nb_al, arena_box['n'])
            arena_top[0] = off + nb_al

            def release(o=off):
                arena_top[0] = o
            stack.callback(release)
            v = arena_box['a'][:, off // 2:off // 2 + nbytes // 2]
            if dt == F32:
                v = v.bitcast(F32)
            if len(shape) > 2:
                names = ["d%d" % i for i in range(len(shape) - 1)]
                pat = "p (%s) -> p %s" % (" ".join(names), " ".join(names))
                v = v.rearrange(pat, **{nm: int(sz) for nm, sz in zip(names, shape[1:])})
            return v

        psf = [st.enter_context(nc.psum_tensor("ps%d" % i, [128, 512], F32)) for i in range(7)]
        psb = st.enter_context(nc.psum_tensor("psb", [128, 1024], BF16))
        pcnt = [0]

        nrot = [5]

        def pnext():
            t = psf[pcnt[0] % nrot[0]]
            pcnt[0] += 1
            return t

        x32 = sb("x32", [128, 8, 1024])
        hT = sb("hT", [128, 8, 1024], BF16)
        brT = sb("brT", [128, 12, 1024], BF16)
        NSLOT = 4
        wring = [sb("wr%d" % i, [128, 4096], BF16) for i in range(NSLOT)]
        wcnt = [0]
        identf = sb("identf", [128, 128])
        identb = sb("identb", [128, 128], BF16)
        masks = sb("masksb", [128, 4, 128])
        onesf = sb("onesf", [128, 128])
        onesb = sb("onesb", [128, 128], BF16)
        rowp = sb("rowpt", [128, NROW])
        colp = sb("colpt", [128, 2 * NCOL])
        ccs = sb("ccs", [128, 8])
        sc32 = sb("sc32", [128, 8])
        scb = sb("scb", [128, 8], BF16)
        screp = sb("screp", [128, 8, 128], BF16)
        gate_bc = sb("gate_bc", [128, 1024])
        modcol = sb("modcol", [128, 16])
        gmod = sb("gmod", [128, 8])
        keep = sb("keept", [128, 1])
        rcos = sb("rcos", [128, 10, 32])
        rsin = sb("rsin", [128, 10, 32])
        stat = sb("stat", [128, 64])
        scr = sb("scr", [128, 1024])
        scr2 = sb("scr2", [128, 1024])

        def wslot():
            t = wring[wcnt[0] % NSLOT]
            wcnt[0] += 1
            return t

        pref = {}

        def load_w(src2d, kch, ncols, slot=None, key=None):
            if key is not None and key in pref:
                return pref.pop(key)
            t = wslot() if slot is None else wring[slot]
            v = t[:, 0:kch * ncols].rearrange("p (k c) -> p k c", k=kch)
            DMA('pool', v, src2d.rearrange("(k p) c -> p k c", p=128))
            return v

        def prefetch(key, src2d, kch, ncols, slot):
            pref[key] = load_w(src2d, kch, ncols, slot=slot)

        DMA('sp', identf[:], identd)
        DMA('sp', masks[:], masksd)
        DMA('sp', colp[:], colpd)
        DMA('sp', ccs[:], ccol)
        DMA('sp', keep[:], keepd)
        DMA('sp', rcos[:], ropec)
        DMA('sp', rsin[:], ropes)
        xv = xin.rearrange("(t p) d -> p t d", p=128)
        for tb in range(8):
            DMA('sp' if tb % 2 == 0 else 'act', x32[:, tb, :], xv[:, tb, :])
        I('dve', 'memset', ap=onesf[:], constant=1.0)
        I('dve', 'memset', ap=onesb[:], constant=1.0)
        CP('dve', identb[:], identf[:])
        ACT(sc32[:], ccs[:], AF.Silu)
        CP('dve', scb[:], sc32[:])
        CP('dve', screp[:], sc32[:].unsqueeze(2).to_broadcast([128, 8, 128]))


        def do_mod_norm(l):
            cb0 = l * NCOL
            ng_col = colp[:, cb0 + C_NG:cb0 + C_NG + 8]
            bmod_col = colp[:, cb0 + C_BMOD:cb0 + C_BMOD + 24]
            DMA('sp', rowp[:], rowpd[l:l + 1, :].to_broadcast([128, NROW]))
            DMA('sp', gate_bc[:], bgate[l:l + 1, :].to_broadcast([128, 1024]))
            import os
            KSUB = int(os.environ.get("KSUB", "99"))
            if KSUB <= 0:
                return
            pmod = pnext()
            for piece in range(4):
                wv = load_w(w_mod[l][:, piece * 512:(piece + 1) * 512], 8, 512, slot=(3 + piece) % 4, key=('mod', l, piece))
                for j in range(4):
                    col = piece * 4 + j
                    for kc in range(8):
                        MM(pmod[:, col:col + 1], wv[:, kc, j * 128:(j + 1) * 128], scb[:, kc:kc + 1],
                           start=(kc == 0), stop=(kc == 7))
            TT('dve', modcol[:], pmod[:, 0:16], bmod_col[:, 0:16], ALU.add)
            STT('dve', gmod[:], modcol[:, 8:16], 1.0, ng_col, ALU.add, ALU.mult)
            if KSUB <= 1:
                return
            for n in range(2):
                wv = load_w(w_mod[l][:, 2048 + n * 512:2048 + (n + 1) * 512], 8, 512, slot=(3 + 4 + n) % 4, key=('mod', l, 4 + n))
                pg = pnext()
                for kc in range(8):
                    MM(pg[:], screp[:, kc, :], wv[:, kc, :], start=(kc == 0), stop=(kc == 7))
                TT('dve', gate_bc[:, n * 512:(n + 1) * 512], pg[:], gate_bc[:, n * 512:(n + 1) * 512], ALU.add)
            if STAGE_A:
                prefetch(('za', l), w_in[l][:, O_ZA:O_ZA + 512], 8, 512, 1)
                prefetch(('wA0', l), w_in[l][:, 0:512], 8, 512, 2)
                prefetch(('wA1', l), w_in[l][:, 512:672], 8, 160, 3)
            with contextlib.ExitStack() as ns:
                sqb = [sb("n_sq", [128, 1024], F32, ns) for _ in range(2)]
                xsb = [sb("n_xs", [128, 1024], BF16, ns) for _ in range(3)]
                for tb in range(8):
                    sq_ = sqb[tb % 2]
                    xs_ = xsb[tb % 3]
                    ACT(sq_[:], x32[:, tb, :], AF.Square)
                    RSUM('dve', stat[:, tb:tb + 1], sq_[:])
                    rstd_from_ssq(stat[:, 8 + tb:9 + tb], stat[:, tb:tb + 1], 1024.0)
                    ACT(xs_[:], x32[:, tb, :], AF.Copy, scale=stat[:, 8 + tb:9 + tb])
                    pa, pb = pnext(), pnext()
                    pvs = (pa[:].bitcast(BF16), pb[:].bitcast(BF16))
                    for kc in range(8):
                        TR(pvs[kc // 4][:, (kc % 4) * 128:(kc % 4 + 1) * 128], xs_[:, kc * 128:(kc + 1) * 128], identb[:])
                    for kc in range(8):
                        src = pvs[kc // 4][:, (kc % 4) * 128:(kc % 4 + 1) * 128]
                        dst = hT[:, kc, tb * 128:(tb + 1) * 128]
                        if kc < 4:
                            TS('dve', dst, src, gmod[:, kc:kc + 1], ALU.mult, modcol[:, kc:kc + 1], ALU.add)
                        else:
                            ACT(dst, src, AF.Identity, scale=gmod[:, kc:kc + 1], bias=modcol[:, kc:kc + 1])

        def silu_zT(l, col0, br0, slot=None, key=None):
            wz = load_w(w_in[l][:, col0:col0 + 512], 8, 512, slot=slot, key=key)
            for j in range(4):
                for hf in range(2):
                    pz = pnext()
                    for kc in range(8):
                        MM(pz[:], wz[:, kc, j * 128:(j + 1) * 128], hT[:, kc, hf * 512:(hf + 1) * 512],
                           start=(kc == 0), stop=(kc == 7))
                    ACT(brT[:, br0 + j, hf * 512:(hf + 1) * 512], pz[:], AF.Silu)

        def gelu_from_psum(dst, ps, t2):
            ACT(dst, ps, AF.Gelu_apprx_tanh)

        def phase_c_gen(l, cs, slots):
            cb0 = l * NCOL
            bs_col = colp[:, cb0 + C_BS:cb0 + C_BS + 4]
            wsT = sb("wsT", [128, 4, 128], BF16, cs)
            gu = sb("c_gu", [128, 512], F32, cs)
            gv = sb("c_gv", [128, 512], F32, cs)
            ct2 = sb("c_t2", [128, 512], F32, cs)
            vn = sb("c_vn", [128, 512], BF16, cs)
            oc = sb("c_oc", [128, 512], BF16, cs)
            cst = sb("c_st", [128, 4], F32, cs)
            DMA('pool', wsT[:], w_sT[l])
            silu_zT(l, O_ZC, 8, slot=slots[0], key=('zc', l))
            yield
            wcu = load_w(w_in[l][:, O_CU:O_CU + 512], 8, 512, slot=slots[0])
            wcv = load_w(w_in[l][:, O_CV:O_CV + 512], 8, 512, slot=slots[1])
            lng = rowp[:, R_LNG:R_LNG + 512]
            lnb = rowp[:, R_LNB:R_LNB + 512]
            for tb in range(8):
                pcu = pnext()
                for kc in range(8):
                    MM(pcu[:], hT[:, kc, tb * 128:(tb + 1) * 128], wcu[:, kc, :], start=(kc == 0), stop=(kc == 7))
                gelu_from_psum(gu[:], pcu[:], ct2[:])
                yield
                pcv = pnext()
                for kc in range(8):
                    MM(pcv[:], hT[:, kc, tb * 128:(tb + 1) * 128], wcv[:, kc, :], start=(kc == 0), stop=(kc == 7))
                gelu_from_psum(gv[:], pcv[:], ct2[:])
                yield
                s_sum, s_mu, s_ss, s_rs = (cst[:, 0:1], cst[:, 1:2], cst[:, 2:3], cst[:, 3:4])
                RSUM('dve', s_sum, gv[:])
                TS('dve', s_mu, s_sum, -1.0 / 512.0, ALU.mult)
                ACT(gv[:], gv[:], AF.Identity, bias=s_mu)
                ACT(ct2[:], gv[:], AF.Square)
                RSUM('dve', s_ss, ct2[:])
                rstd_from_ssq(s_rs, s_ss, 512.0)
                STT('dve', gv[:], gv[:], s_rs, lng, ALU.mult, ALU.mult)
                TT('dve', vn[:], gv[:], lnb, ALU.add)
                yield
                psv = pnext()
                for g in range(4):
                    MM(psv[:, g * 128:(g + 1) * 128], wsT[:, g, :], vn[:, g * 128:(g + 1) * 128])
                for g in range(4):
                    STT('dve', oc[:, g * 128:(g + 1) * 128], psv[:, g * 128:(g + 1) * 128], bs_col[:, g:g + 1],
                        gu[:, g * 128:(g + 1) * 128], ALU.add, ALU.mult)
                yield
                for j in range(4):
                    TR(psb[:, j * 128:(j + 1) * 128], oc[:, j * 128:(j + 1) * 128], identb[:])
                TT('dve', brT[:, 8:12, tb * 128:(tb + 1) * 128],
                   psb[:, 0:512].rearrange("p (a b) -> p a b", a=4),
                   brT[:, 8:12, tb * 128:(tb + 1) * 128], ALU.mult)
                yield

        def interleave(gens):
            gens = list(gens)
            while gens:
                for g_ in list(gens):
                    try:
                        next(g_)
                    except StopIteration:
                        gens.remove(g_)

        def phase_merge(l):
            with contextlib.ExitStack() as cs:
                mT32 = sb("mT32", [128, 8, 1024], F32, cs)
                mT = sb("mT", [128, 8, 1024], BF16, cs)
                gsb = sb("gsb", [128, 2, 512], F32, cs)
                tmp = sb("mtmp", [128, 2, 512], F32, cs)
                pieces = []
                for n in range(3):
                    pieces.append((w_br[l, n], 4, 1024))
                    for hh in range(2):
                        pieces.append((w_in[l][:, O_GL + n * 1024 + hh * 512:O_GL + n * 1024 + (hh + 1) * 512], 8, 512))
                for hh in range(2):
                    pieces.append((w_o[l][:, hh * 512:(hh + 1) * 512], 8, 512))
                loaded = {}
                PF = 2

                def need(k):
                    for kk in range(len(loaded), min(k + PF, len(pieces))):
                        src, kch, ncol = pieces[kk]
                        loaded[kk] = load_w(src, kch, ncol, slot=kk % NSLOT)
                    return loaded[k]

                for n in range(3):
                    wbr = need(3 * n)
                    for oc_ in range(8):
                        wgp = need(3 * n + 1 + oc_ // 4)
                        for hf in range(2):
                            pyb = pnext()
                            pgl = pnext()
                            for kc in range(4):
                                MM(pyb[:], wbr[:, kc, oc_ * 128:(oc_ + 1) * 128], brT[:, n * 4 + kc, hf * 512:(hf + 1) * 512],
                                   start=(kc == 0), stop=(kc == 3))
                            for kc in range(8):
                                MM(pgl[:], wgp[:, kc, (oc_ % 4) * 128:(oc_ % 4 + 1) * 128],
                                   hT[:, kc, hf * 512:(hf + 1) * 512], start=(kc == 0), stop=(kc == 7))
                            ACT(gsb[:, hf, :], pgl[:], AF.Sigmoid)
                            msl = mT32[:, oc_, hf * 512:(hf + 1) * 512]
                            if n == 0:
                                TT('dve', msl, pyb[:], gsb[:, hf, :], ALU.mult)
                            else:
                                TT('dve', tmp[:, hf, :], pyb[:], gsb[:, hf, :], ALU.mult)
                                if n == 1:
                                    TT('dve', msl, msl, tmp[:, hf, :], ALU.add)
                                else:
                                    TT('dve', mT[:, oc_, hf * 512:(hf + 1) * 512], msl, tmp[:, hf, :], ALU.add)
                wo = [need(9), need(10)]
                if l + 1 < 2:
                    prefetch(('mod', l + 1, 0), w_mod[l + 1][:, 0:512], 8, 512, 3)
                    prefetch(('mod', l + 1, 1), w_mod[l + 1][:, 512:1024], 8, 512, 0)
                for tb in range(8):
                    for hh in range(2):
                        py = pnext()
                        for kc in range(8):
                            MM(py[:], mT[:, kc, tb * 128:(tb + 1) * 128], wo[hh][:, kc, :], start=(kc == 0), stop=(kc == 7))
                        TT('dve', tmp[:, hh, :], py[:], gate_bc[:, hh * 512:(hh + 1) * 512], ALU.mult)
                        TT('dve', x32[:, tb, hh * 512:(hh + 1) * 512], x32[:, tb, hh * 512:(hh + 1) * 512], tmp[:, hh, :], ALU.add)

        def rope(xv, blk, outv, rt, ru):
            Cb = rcos[:, blk, :].unsqueeze(1).to_broadcast([128, 8, 32])
            TT('dve', rt[:], xv, Cb, ALU.mult)
            x5 = xv.rearrange("p h (a t i) -> p h a t i", a=2, t=2)
            u5 = ru[:].rearrange("p h (a t i) -> p h a t i", a=2, t=2)
            S4 = rsin[:, blk, :].rearrange("p (a t i) -> p a t i", a=2, t=2)
            for t_ in range(2):
                Sb_ = S4[:, :, t_, :].unsqueeze(1).to_broadcast([128, 8, 2, 8])
                TT('dve', u5[:, :, :, t_, :], x5[:, :, :, 1 - t_, :], Sb_, ALU.mult)
            TT('dve', outv, rt[:], ru[:], ALU.add)

        def rope2d(xv, blk, outv, rt_, ru_):
            TT('dve', rt_, xv, rcos[:, blk, :], ALU.mult)
            x4 = xv.rearrange("p (a t i) -> p a t i", a=2, t=2)
            u4 = ru_.rearrange("p (a t i) -> p a t i", a=2, t=2)
            S4 = rsin[:, blk, :].rearrange("p (a t i) -> p a t i", a=2, t=2)
            for t_ in range(2):
                TT('dve', u4[:, :, t_, :], x4[:, :, 1 - t_, :], S4[:, :, t_, :], ALU.mult)
            TT('dve', outv, rt_, ru_, ALU.add)

        def phase_a(l):
            with contextlib.ExitStack() as cs:
                qT = sb("qT", [128, 8, 1024], BF16, cs)
                kT = sb("kT", [128, 8, 1280], BF16, cs)
                vx = sb("vx", [128, 10, 8, 65], BF16, cs)
                cT = [sb("cT%d" % i, [128, 5, 128], BF16, cs) for i in range(2)]
                cqn = [sb("cqn%d" % i, [128, 384], BF16, cs) for i in range(2)]
                ckv32 = [sb("ckv32_%d" % i, [128, 256], F32, cs) for i in range(2)]
                kr32 = [sb("kr32_%d" % i, [128, 32], F32, cs) for i in range(2)]
                ckvb = [sb("ckvb%d" % i, [128, 256], BF16, cs) for i in range(2)]
                cq32 = sb("cq32", [128, 384], F32, cs)
                qn32 = sb("qn32", [128, 8, 96], F32, cs)
                qf = sb("qf", [128, 8, 96], BF16, cs)
                rt = sb("rt", [128, 8, 32], F32, cs)
                ru = sb("ru", [128, 8, 32], F32, cs)
                kn32 = sb("kn32", [128, 8, 96], F32, cs)
                kf = sb("kf", [128, 8, 96], BF16, cs)
                krr = [sb("krr%d" % i, [128, 32], F32, cs) for i in range(2)]
                krg = sb("krg", [128, 32], F32, cs)
                krt = sb("krt", [128, 32], F32, cs)
                kru = sb("kru", [128, 32], F32, cs)
                kscr = sb("kscr", [128, 544], F32, cs)
                PT = [sb("PT%d" % i, [128, 512], BF16, cs) for i in range(3)]
                opair = sb("opair", [128, 4, 128], BF16, cs)
                rden = sb("rden", [128, 4], F32, cs)
                qanrow = rowp[:, R_QAN:R_QAN + 384]
                kvnrow = rowp[:, R_KVN:R_KVN + 256]
                qnrow = rowp[:, R_QN:R_QN + 96]
                knrow = rowp[:, R_KN:R_KN + 96]
                for h in range(8):
                    DMA('pool', qT[96:100, h, :], qseg)
                    DMA('pool', kT[96:100, h, :], kpen)
                I('dve', 'memset', ap=vx[:, :, :, 64:65], constant=1.0)
                wuq = load_w(w_uq[l], 3, 768, slot=0)
                silu_zT(l, O_ZA, 0, slot=1, key=('za', l))
                wA0 = load_w(w_in[l][:, 0:512], 8, 512, slot=2, key=('wA0', l))
                wA1 = load_w(w_in[l][:, 512:672], 8, 160, slot=3, key=('wA1', l))
                wukv = load_w(w_ukv[l], 2, 1024, slot=1)

                def kr_side(i, st_, k32):
                    ACT(kscr[:, 512:544], k32[:], AF.Square)
                    RSUM('dve', stat[:, 32 + st_:33 + st_], kscr[:, 512:544])
                    TT('dve', krg[:], k32[:], knrow[:, 64:96], ALU.mult)
                    rope2d(krg[:], i, krr[st_][:], krt[:], kru[:])

                def F_gen(i):
                    st_ = i % 2
                    c32, k32 = ckv32[st_], kr32[st_]
                    if i < 2:
                        DMA('sp', c32[:], cck[l, i * 128:(i + 1) * 128, :])
                        DMA('sp', k32[:], ckr[l, i * 128:(i + 1) * 128, :])
                        CP('dve', ckvb[st_][:], c32[:])
                        yield
                        for c_ in range(2):
                            TR(psb[:, c_ * 128:(c_ + 1) * 128], ckvb[st_][:, c_ * 128:(c_ + 1) * 128], identb[:])
                        CP('dve', cT[st_][:, 3:5, :], psb[:, 0:256].rearrange("p (a b) -> p a b", a=2))
                        yield
                        kr_side(i, st_, k32)
                        yield
                        return
                    tb = i - 2
                    pA0, pA1 = pnext(), pnext()
                    for kc in range(8):
                        MM(pA0[:], hT[:, kc, tb * 128:(tb + 1) * 128], wA0[:, kc, :], start=(kc == 0), stop=(kc == 7))
                    for kc in range(8):
                        MM(pA1[:, 0:160], hT[:, kc, tb * 128:(tb + 1) * 128], wA1[:, kc, :], start=(kc == 0), stop=(kc == 7))
                    I('act', 'copy', out=cq32[:], in_=pA0[:, 0:384])
                    I('act', 'copy', out=c32[:, 0:128], in_=pA0[:, 384:512])
                    I('act', 'copy', out=c32[:, 128:256], in_=pA1[:, 0:128])
                    CP('dve', k32[:], pA1[:, 128:160])
                    yield
                    ACT(scr2[:, 0:384], cq32[:], AF.Square)
                    RSUM('dve', stat[:, 20:21], scr2[:, 0:384])
                    rstd_from_ssq(stat[:, 21:22], stat[:, 20:21], 384.0)
                    STT('dve', cqn[st_][:], cq32[:], stat[:, 21:22], qanrow, ALU.mult, ALU.mult)
                    yield
                    ACT(scr2[:, 384:640], c32[:], AF.Square)
                    RSUM('dve', stat[:, 22:23], scr2[:, 384:640])
                    rstd_from_ssq(stat[:, 23:24], stat[:, 22:23], 256.0)
                    STT('dve', c32[:], c32[:], stat[:, 23:24], kvnrow, ALU.mult, ALU.mult)
                    CP('dve', ckvb[st_][:], c32[:])
                    DMA('sp', nckv_o[l, tb * 128:(tb + 1) * 128, :], c32[:])
                    DMA('sp', nkr_o[l, tb * 128:(tb + 1) * 128, :], k32[:])
                    yield
                    for c_ in range(3):
                        TR(psb[:, c_ * 128:(c_ + 1) * 128], cqn[st_][:, c_ * 128:(c_ + 1) * 128], identb[:])
                    for c_ in range(2):
                        TR(psb[:, (3 + c_) * 128:(4 + c_) * 128], ckvb[st_][:, c_ * 128:(c_ + 1) * 128], identb[:])
                    I('act', 'copy', out=cT[st_][:], in_=psb[:, 0:640].rearrange("p (a b) -> p a b", a=5))
                    yield
                    kr_side(i, st_, k32)
                    yield

                def Q_gen(i):
                    st_ = i % 2
                    tb = i - 2
                    pq0, pq1 = pnext(), pnext()
                    for c_ in range(3):
                        MM(pq0[:, 0:480], cT[st_][:, c_, :], wuq[:, c_, 0:480], start=(c_ == 0), stop=(c_ == 2))
                    for c_ in range(3):
                        MM(pq1[:, 0:288], cT[st_][:, c_, :], wuq[:, c_, 480:768], start=(c_ == 0), stop=(c_ == 2))
                    I('act', 'copy', out=qn32[:, 0:5, :], in_=pq0[:, 0:480].rearrange("p (h d) -> p h d", h=5))
                    I('act', 'copy', out=qn32[:, 5:8, :], in_=pq1[:, 0:288].rearrange("p (h d) -> p h d", h=3))
                    yield
                    ACT(scr[:, 0:768].rearrange("p (h d) -> p h d", h=8), qn32[:], AF.Square)
                    RSUM('dve', stat[:, 48:56], scr[:, 0:768].rearrange("p (h d) -> p h d", h=8))
                    TT('dve', qn32[:], qn32[:], qnrow.unsqueeze(1).to_broadcast([128, 8, 96]), ALU.mult)
                    ACT(stat[:, 56:64], stat[:, 48:56], AF.Sqrt, bias=EPS, scale=1.0 / 96.0)
                    yield
                    rope(qn32[:, :, 64:96], i, qn32[:, :, 64:96], rt, ru)
                    yield
                    I('dve', 'reciprocal', out=stat[:, 56:64], in_=stat[:, 56:64])
                    TT('dve', qf[:], qn32[:], stat[:, 56:64].unsqueeze(2).to_broadcast([128, 8, 96]), ALU.mult)
                    yield
                    for h in range(8):
                        TR(psb[0:96, h * 128:(h + 1) * 128], qf[:, h, :], identb[:])
                    I('act', 'copy', out=qT[0:96, :, tb * 128:(tb + 1) * 128], in_=psb[0:96, :].rearrange("p (h t) -> p h t", h=8))
                    yield

                def K_gen(i):
                    st_ = i % 2
                    kr = kr32[st_]
                    pk = [pnext(), pnext()]
                    for b_ in range(2):
                        for c_ in range(2):
                            MM(pk[b_][:], cT[st_][:, 3 + c_, :], wukv[:, c_, b_ * 512:(b_ + 1) * 512], start=(c_ == 0), stop=(c_ == 1))
                    for b_ in range(2):
                        pv = pk[b_][:].rearrange("p (h d) -> p h d", h=4)
                        I('act', 'copy', out=vx[:, i, b_ * 4:(b_ + 1) * 4, 0:64], in_=pv[:, :, 64:128])
                        CP('dve', kn32[:, b_ * 4:(b_ + 1) * 4, 0:64], pv[:, :, 0:64])
                    yield
                    ACT(kscr[:, 0:512].rearrange("p (h d) -> p h d", h=8), kn32[:, :, 0:64], AF.Square)
                    RSUM('dve', stat[:, 24:32], kscr[:, 0:512].rearrange("p (h d) -> p h d", h=8))
                    TS('dve', stat[:, 24:32], stat[:, 24:32], stat[:, 32 + st_:33 + st_], ALU.add)
                    ACT(stat[:, 40:48], stat[:, 24:32], AF.Sqrt, bias=EPS, scale=1.0 / 96.0)
                    TT('dve', kn32[:, :, 0:64], kn32[:, :, 0:64], knrow[:, 0:64].unsqueeze(1).to_broadcast([128, 8, 64]), ALU.mult)
                    yield
                    I('dve', 'reciprocal', out=stat[:, 40:48], in_=stat[:, 40:48])
                    TT('dve', kf[:, :, 0:64], kn32[:, :, 0:64], stat[:, 40:48].unsqueeze(2).to_broadcast([128, 8, 64]), ALU.mult)
                    TT('dve', kf[:, :, 64:96], krr[st_][:].unsqueeze(1).to_broadcast([128, 8, 32]),
                       stat[:, 40:48].unsqueeze(2).to_broadcast([128, 8, 32]), ALU.mult)
                    yield
                    for h in range(8):
                        TR(psb[0:96, h * 128:(h + 1) * 128], kf[:, h, :], identb[:])
                    CP('dve', kT[0:96, :, i * 128:(i + 1) * 128], psb[0:96, :].rearrange("p (h t) -> p h t", h=8))
                    yield

                interleave([F_gen(0)])
                for i in range(10):
                    gens = [K_gen(i)]
                    if i >= 2:
                        gens.append(Q_gen(i))
                    if i + 1 <= 9:
                        gens.append(F_gen(i + 1))
                    interleave(gens)
                if STAGE_B:
                    prefetch(('zb', l), w_in[l][:, O_ZB:O_ZB + 512], 8, 512, 0)
                    prefetch(('qkv0', l), w_in[l][:, O_QKV:O_QKV + 512], 8, 512, 1)
                    if STAGE_C:
                        prefetch(('zc', l), w_in[l][:, O_ZC:O_ZC + 512], 8, 512, 2)
                sc_ = 1.0 / math.sqrt(96.0)
                items = [(hp, hf, hl, kb) for hp in range(4) for hf in range(2) for hl in range(2) for kb in range(10)]
                LOOK = 2
                pS_q = []

                def emit_S(idx):
                    hp, hf, hl, kb = items[idx]
                    h = hp * 2 + hl
                    pS = pnext()
                    MM(pS[:], kT[0:100, h, kb * 128:(kb + 1) * 128], qT[0:100, h, hf * 512:(hf + 1) * 512])
                    pS_q.append(pS)

                for i0 in range(min(LOOK, len(items))):
                    emit_S(i0)
                for idx, (hp, hf, hl, kb) in enumerate(items):
                    h = hp * 2 + hl
                    po = psf[5 + hl]
                    pov = po[:, 0:260].rearrange("p (j e) -> p j e", j=4)
                    if kb == 0:
                        I('dve', 'memset', ap=po[:, 0:260], constant=0.0)
                    pS = pS_q.pop(0)
                    if idx + LOOK < len(items):
                        emit_S(idx + LOOK)
                    pt = PT[idx % 3]
                    ACT(pt[:], pS[:], AF.Exp, scale=sc_)
                    for j in range(4):
                        I('pe', 'matmul', out=pov[:, j, :], lhsT=pt[:, j * 128:(j + 1) * 128], rhs=vx[:, kb, h, :],
                          start=False, stop=(kb == 9), skip_group_check=True)
                    if kb == 9:
                        I('dve', 'reciprocal', out=rden[:], in_=pov[:, :, 64])
                        TT('dve', opair[:, :, hl * 64:(hl + 1) * 64], pov[:, :, 0:64],
                           rden[:].unsqueeze(2).to_broadcast([128, 4, 64]), ALU.mult)
                        if hl == 1:
                            for j in range(4):
                                TR(psb[:, j * 128:(j + 1) * 128], opair[:, j, :], identb[:])
                            TT('dve', brT[:, hp, hf * 512:(hf + 1) * 512], psb[:, 0:512], brT[:, hp, hf * 512:(hf + 1) * 512], ALU.mult)

        def phase_b(l):
            cb0 = l * NCOL
            nrot[0] = 7
            with contextlib.ExitStack() as cs:
                qTb = sb("qTb", [128, 4, 1024], BF16, cs)
                kTb = sb("kTb", [128, 4, 1024], BF16, cs)
                vtok = sb("vtok", [128, 8, 512], BF16, cs)
                oacc = sb("oacc", [128, 8, 512], F32, cs)
                gab = sb("gab", [128, 8, 16], F32, cs)
                gg = sb("gg", [128, 8, 8], F32, cs)
                bet = sb("bet", [128, 8, 8], F32, cs)
                gt1 = scr[:, 0:64].rearrange("p (a b) -> p a b", a=8)
                gt2 = scr[:, 64:128].rearrange("p (a b) -> p a b", a=8)
                ea = scr[:, 128:136]
                Gb = scr[:, 0:1024].rearrange("p (a b) -> p a b", a=8)
                DTi = scr2[:, 0:512].rearrange("p (a b) -> p a b", a=4)
                tmpf = scr2[:, 512:1024].rearrange("p (a b) -> p a b", a=4)
                acc = scr2[:, 0:1024].rearrange("p (s t) -> p s t", s=4)

                def v4(ps):
                    return ps[:].rearrange("p (a b) -> p a b", a=4)

                b1 = contextlib.ExitStack()
                xps = [sb("xp%d" % i, [128, 4, 260], BF16, b1) for i in range(2)]
                dgs = [sb("dg%d" % i, [128, 5, 128], BF16, b1) for i in range(2)]
                accs = [scr2, sb("acc1", [128, 1024], F32, b1)]
                sqs = [scr, sb("sq1", [128, 1032], F32, b1)]
                vtmp = sb("vtmp", [128, 1024], BF16, b1)
                I('dve', 'memset', ap=oacc[:], constant=0.0)
                for xp_ in xps:
                    I('dve', 'memset', ap=xp_[:], constant=0.0)

                def b1_gen():
                    silu_zT(l, O_ZB, 4, slot=0, key=('zb', l))
                    yield
                    for pi in range(3):
                        wq = load_w(w_in[l][:, O_QKV + pi * 512:O_QKV + (pi + 1) * 512], 8, 512, slot=(pi + 1) % 2, key=('qkv%d' % pi, l))
                        for j in range(4):
                            ch = pi * 4 + j
                            par = ch % 2
                            xp = xps[par]
                            dg = dgs[par]
                            accf = accs[par][:, 0:1024]
                            sq = sqs[par][:, 0:1024]
                            for hf in range(2):
                                pz = pnext()
                                for kc in range(8):
                                    MM(pz[:], wq[:, kc, j * 128:(j + 1) * 128], hT[:, kc, hf * 512:(hf + 1) * 512],
                                       start=(kc == 0), stop=(kc == 7))
                                I('act', 'copy', out=xp[:, 2 * hf:2 * hf + 2, 2:258], in_=pz[:].rearrange("p (s t) -> p s t", s=2))
                            yield
                            TS('dve', xp[:, 1:4, 0:2], xp[:, 0:3, 256:258], keep[:, 0:1], ALU.mult)
                            TS('dve', xp[:, 0:3, 258:260], xp[:, 1:4, 2:4], keep[:, 0:1], ALU.mult)
                            cw = colp[:, cb0 + C_CONV + ch * 5:cb0 + C_CONV + ch * 5 + 5]
                            TT('dve', dg[:], identf[:].unsqueeze(1).to_broadcast([128, 5, 128]),
                               cw.unsqueeze(2).to_broadcast([128, 5, 128]), ALU.mult)
                            yield
                            for h2 in range(2):
                                pc = pnext()
                                for sg in range(2):
                                    for jj in range(5):
                                        MM(pc[:, sg * 256:(sg + 1) * 256], dg[:, jj, :], xp[:, h2 * 2 + sg, jj:jj + 256],
                                           start=(jj == 0), stop=(jj == 4))
                                if pi == 2:
                                    ACT(vtmp[:, h2 * 512:(h2 + 1) * 512], pc[:], AF.Silu)
                                else:
                                    ACT(accf[:, h2 * 512:(h2 + 1) * 512], pc[:], AF.Silu)
                            yield
                            if pi == 2:
                                for tb in range(8):
                                    TR(psb[:, tb * 128:(tb + 1) * 128], vtmp[:, tb * 128:(tb + 1) * 128], identb[:])
                                CP('dve', vtok[:, :, j * 128:(j + 1) * 128], psb[:].rearrange("p (t d) -> p t d", t=8))
                                yield
                            else:
                                ACT(sq, accf, AF.Square)
                                dstT = qTb if pi == 0 else kTb
                                for hf in range(2):
                                    pn = pnext()
                                    MM(pn[:], onesf[:], sq[:, hf * 512:(hf + 1) * 512])
                                    ACT(sq[:, hf * 512:(hf + 1) * 512], pn[:], AF.Sqrt, bias=EPS, scale=1.0)
                                yield
                                I('dve', 'reciprocal', out=sq, in_=sq)
                                STT('dve', dstT[:, j, :], accf, (128.0 ** -0.5) if pi == 0 else 1.0, sq, ALU.mult, ALU.mult)
                                yield

                if STAGE_C:
                    interleave([b1_gen(), phase_c_gen(l, b1, (2, 3))])
                else:
                    interleave([b1_gen()])
                b1.close()
                S32 = [sb("S32_%d" % i, [128, 4, 128], F32, cs) for i in range(2)]
                Sbb = [sb("Sbb%d" % i, [128, 4, 128], BF16, cs) for i in range(2)]

                def f4(ap2d):
                    return ap2d.rearrange("p (a b) -> p a b", a=4)
                twoI = sb("twoI", [128, 128], F32, cs)
                TS('dve', twoI[:], identf[:], 2.0, ALU.mult)
                sets = []
                s0_ = dict(
                    gst=sb("gst0", [128, 16], F32, cs), gsm=[sb("gsm0_%d" % i, [128, 40], F32, cs) for i in range(2)],
                    Gb=f4(scr[:, 0:512]), DTi=f4(scr2[:, 0:512]), tmpf=f4(scr2[:, 512:1024]),
                    ktok=sb("ktok0", [128, 4, 128], BF16, cs)[:],
                    Mb=[sb("Mb0_%d" % i, [128, 4, 128], F32, cs)[:] for i in range(2)],
                    Nb=[sb("Nb0_%d" % i, [128, 4, 128], F32, cs)[:] for i in range(2)],
                    Pb=sb("Pb0", [128, 4, 128], F32, cs)[:], Rb=sb("Rb0", [128, 4, 128], BF16, cs)[:],
                    kgb=sb("kgb0", [128, 4, 128], BF16, cs)[:], vnb=sb("vnb0", [128, 4, 128], BF16, cs)[:],
                    tmp2=sb("tmp20", [128, 4, 128], F32, cs)[:],
                    attT=[sb("attT0_%d" % i, [128, 4, 128], BF16, cs)[:] for i in range(2)],
                    kpb=[sb("kpb0_%d" % i, [128, 4, 128], BF16, cs)[:] for i in range(2)],
                    u32=[sb("u32_0_%d" % i, [128, 4, 128], F32, cs)[:] for i in range(2)],
                    wTb=[sb("wTb0_%d" % i, [128, 4, 128], BF16, cs)[:] for i in range(2)])
                sets.append(s0_)
                wpos = [0, 0]

                def carve(nbytes, f32):
                    if wpos[1] + nbytes > 8192:
                        wpos[0] += 1
                        wpos[1] = 0
                    t = wring[wpos[0]]
                    e0 = wpos[1] // 2
                    v = t[:, e0:e0 + nbytes // 2]
                    wpos[1] += nbytes
                    if f32:
                        v = v.bitcast(F32)
                    return f4(v)
                s1_ = dict(gst=sb("gst1", [128, 16], F32, cs), gsm=[sb("gsm1_%d" % i, [128, 40], F32, cs) for i in range(2)])
                for nm in ("Gb", "DTi", "tmpf", "Pb", "tmp2"):
                    s1_[nm] = carve(2048, True)
                u32a = carve(2048, True)
                s1_["Mb"] = [carve(2048, True) for _ in range(2)]
                s1_["Nb"] = [carve(2048, True) for _ in range(2)]
                for nm in ("ktok", "Rb", "kgb", "vnb"):
                    s1_[nm] = carve(1024, False)
                for nm in ("attT", "kpb", "wTb"):
                    s1_[nm] = [carve(1024, False) for _ in range(2)]
                s1_["u32"] = [u32a, carve(2048, True)]
                sets.append(s1_)
                ob32 = s0_["tmp2"]
                obb = s0_["kgb"]

                wgab = load_w(w_in[l][:, O_GA:O_GA + 16], 8, 16)
                pgab = pnext()
                for tb in range(8):
                    for kc in range(8):
                        MM(pgab[:, tb * 16:(tb + 1) * 16], hT[:, kc, tb * 128:(tb + 1) * 128], wgab[:, kc, :],
                           start=(kc == 0), stop=(kc == 7))
                CP('dve', gab[:], pgab[:, 0:128].rearrange("p (t c) -> p t c", t=8))
                dtb = rowp[:, R_DTB:R_DTB + 8].unsqueeze(1).to_broadcast([128, 8, 8])
                TT('dve', gt1[:], gab[:, :, 0:8], dtb, ALU.add)
                TS('dve', gt2[:], gt1[:], -1.0, ALU.mult)
                TT('dve', gt2[:], gt2[:], gt1[:], ALU.max)
                ACT(gt2[:], gt2[:], AF.Exp, scale=-1.0)
                ACT(gt2[:], gt2[:], AF.Ln, bias=1.0, scale=1.0)
                TS('dve', gt1[:], gt1[:], 0.0, ALU.max)
                TT('dve', gt1[:], gt1[:], gt2[:], ALU.add)
                ACT(ea[:], rowp[:, R_ALOG:R_ALOG + 8], AF.Exp)
                TT('dve', gg[:], gt1[:], ea[:].unsqueeze(1).to_broadcast([128, 8, 8]), ALU.mult)
                TS('dve', gg[:], gg[:], -1.0, ALU.mult)
                ACT(bet[:], gab[:, :, 8:16], AF.Sigmoid)

                def prep_gen(c, d, par):
                    T = sets[d]
                    gst, gsm, Gb, DTi, tmpf, ktok = T["gst"], T["gsm"][par], T["Gb"], T["DTi"], T["tmpf"], T["ktok"]
                    Mb, Nb, Pb, Rb, kgb, vnb, tmp2 = T["Mb"], T["Nb"], T["Pb"], T["Rb"], T["kgb"], T["vnb"], T["tmp2"]
                    attT, kpb, u32, wTb = T["attT"][par], T["kpb"][par], T["u32"][par], T["wTb"][par]
                    cs_ = slice(c * 128, (c + 1) * 128)
                    mi, ms = (0, 2) if d == 0 else (1, 3)
                    dsl = slice(d * 4, d * 4 + 4)
                    pgc = pnext()
                    MM(pgc[:, 0:4], masks[:, mi, :], gg[:, c, dsl])
                    MM(pgc[:, 8:12], onesf[:], gg[:, c, dsl])
                    I('act', 'copy', out=gst[:, 0:16], in_=pgc[:, 0:16])
                    gcol = gst[:, 0:4]
                    glb = gst[:, 8:12]
                    egc, ekl, egl, nbeta, t4 = gsm[:, 0:4], gsm[:, 8:12], gsm[:, 16:20], gsm[:, 24:28], gsm[:, 32:36]
                    ACT(egc, gcol, AF.Exp)
                    TT('dve', t4, glb, gcol, ALU.subtract)
                    ACT(ekl, t4, AF.Exp)
                    ACT(egl, glb, AF.Exp)
                    TS('dve', nbeta, bet[:, c, dsl], -1.0, ALU.mult)
                    CP('dve', Gb, gg[:, c, dsl].unsqueeze(2).to_broadcast([128, 4, 128]))
                    yield
                    for h in range(4):
                        TR(psb[:, h * 128:(h + 1) * 128], kTb[:, h, cs_], identb[:])
                    I('act', 'copy', out=ktok, in_=psb[:, 0:512].rearrange("p (a b) -> p a b", a=4))
                    yield
                    pB = pnext()
                    for h in range(4):
                        MM(pB[:, h * 128:(h + 1) * 128], Gb[:, h, :], masks[:, mi, :])
                    for h in range(4):
                        ACT(DTi[:, h, :], pB[:, h * 128:(h + 1) * 128], AF.Relu, scale=-1.0, bias=gcol[:, h:h + 1])
                    ACT(DTi, DTi, AF.Exp, scale=-1.0)
                    TT('dve', DTi, DTi, masks[:, mi, :].unsqueeze(1).to_broadcast([128, 4, 128]), ALU.mult)
                    yield
                    pG = pnext()
                    for h in range(4):
                        MM(pG[:, h * 128:(h + 1) * 128], kTb[:, h, cs_], kTb[:, h, cs_])
                    TT('dve', tmpf, v4(pG), DTi, ALU.mult)
                    TT('dve', tmpf, tmpf, nbeta.unsqueeze(2).to_broadcast([128, 4, 128]), ALU.mult)
                    TT('dve', Mb[0], tmpf, masks[:, ms, :].unsqueeze(1).to_broadcast([128, 4, 128]), ALU.mult)
                    yield
                    pQK = pnext()
                    for h in range(4):
                        MM(pQK[:, h * 128:(h + 1) * 128], kTb[:, h, cs_], qTb[:, h, cs_])
                    TT('dve', attT, v4(pQK), DTi, ALU.mult)
                    TT('dve', Pb, Mb[0], identf[:].unsqueeze(1).to_broadcast([128, 4, 128]), ALU.add)
                    yield
                    pT0 = pnext()
                    for h in range(4):
                        TR(pT0[:, h * 128:(h + 1) * 128], Mb[0][:, h, :], identf[:])
                    idb4 = identf[:].unsqueeze(1).to_broadcast([128, 4, 128])
                    TT('dve', Nb[0], idb4, v4(pT0), ALU.subtract)
                    yield
                    id2 = twoI[:].unsqueeze(1).to_broadcast([128, 4, 128])
                    for k in range(6):
                        for hh in range(2):
                            pE = pnext()
                            for h in (2 * hh, 2 * hh + 1):
                                MM(pE[:, (h % 2) * 128:(h % 2 + 1) * 128], Pb[:, h, :], Nb[0][:, h, :])
                            TT('dve', Nb[1][:, 2 * hh:2 * hh + 2, :], twoI[:].unsqueeze(1).to_broadcast([128, 2, 128]),
                               pE[:, 0:256].rearrange("p (a b) -> p a b", a=2), ALU.subtract)
                        yield
                        for hh in range(2):
                            pX = pnext()
                            for h in (2 * hh, 2 * hh + 1):
                                MM(pX[:, (h % 2) * 128:(h % 2 + 1) * 128], Nb[1][:, h, :], Pb[:, h, :])
                            I('act', 'copy', out=Pb[:, 2 * hh:2 * hh + 2, :], in_=pX[:, 0:256].rearrange("p (a b) -> p a b", a=2))
                        yield
                    I('act', 'copy', out=Rb, in_=Pb)
                    R = Rb
                    TT('dve', kgb, ktok, egc.unsqueeze(2).to_broadcast([128, 4, 128]), ALU.mult)
                    TT('dve', kpb, ktok, ekl.unsqueeze(2).to_broadcast([128, 4, 128]), ALU.mult)
                    yield
                    pU = pnext()
                    for h in range(4):
                        MM(pU[:, h * 128:(h + 1) * 128], R[:, h, :], vtok[:, c, h * 128:(h + 1) * 128])
                    I('act', 'copy', out=u32, in_=v4(pU))
                    yield
                    pW = pnext()
                    for h in range(4):
                        MM(pW[:, h * 128:(h + 1) * 128], kgb[:, h, :], R[:, h, :])
                    I('act', 'copy', out=wTb, in_=v4(pW))
                    yield
                def scan_gen(c, d, par, first):
                    T = sets[d]
                    gsm, vnb, tmp2 = T["gsm"][par], T["vnb"], T["tmp2"]
                    attT, kpb, u32, wTb = T["attT"][par], T["kpb"][par], T["u32"][par], T["wTb"][par]
                    egc, egl = gsm[:, 0:4], gsm[:, 16:20]
                    cs_ = slice(c * 128, (c + 1) * 128)
                    dsl = slice(d * 4, d * 4 + 4)
                    S, Sb_ = S32[d], Sbb[d]
                    seg_start = (c % 2 == 0) if d == 0 else (c % 2 == 1)
                    if seg_start:
                        DMA('sp', tmp2, s0d[l, c // 2, d].rearrange("h k v -> k h v"))
                        if first:
                            CP('dve', S[:], tmp2)
                        else:
                            STT('dve', S[:], S[:], keep[:, 0:1], tmp2, ALU.mult, ALU.add)
                        CP('dve', Sb_[:], S[:])
                        yield
                    pWS = pnext()
                    for h in range(4):
                        MM(pWS[:, h * 128:(h + 1) * 128], wTb[:, h, :], Sb_[:, h, :])
                    TT('dve', tmp2, u32, v4(pWS), ALU.subtract)
                    TT('dve', vnb, tmp2, bet[:, c, dsl].unsqueeze(2).to_broadcast([128, 4, 128]), ALU.mult)
                    yield
                    pQS = pnext()
                    for h in range(4):
                        MM(pQS[:, h * 128:(h + 1) * 128], qTb[:, h, cs_], Sb_[:, h, :])
                    TT('dve', tmp2, v4(pQS), egc.unsqueeze(2).to_broadcast([128, 4, 128]), ALU.mult)
                    yield
                    pAV = pnext()
                    for h in range(4):
                        MM(pAV[:, h * 128:(h + 1) * 128], attT[:, h, :], vnb[:, h, :])
                    TT('dve', tmp2, tmp2, v4(pAV), ALU.add)
                    ov = oacc[:, c, :].rearrange("p (a b) -> p a b", a=4)
                    TT('pool', ov, ov, tmp2, ALU.add)
                    yield
                    pDS = pnext()
                    for h in range(4):
                        MM(pDS[:, h * 128:(h + 1) * 128], kpb[:, h, :], vnb[:, h, :])
                    TT('dve', S[:], S[:], egl.unsqueeze(2).to_broadcast([128, 4, 128]), ALU.mult)
                    TT('dve', S[:], S[:], v4(pDS), ALU.add)
                    CP('dve', Sb_[:], S[:])
                    seg_end = (c % 2 == 1) if d == 0 else (c % 2 == 0)
                    if seg_end:
                        DMA('sp', nst_o[l, c // 2, d].rearrange("h k v -> k h v"), S[:])
                    yield

                def cd(s_, d):
                    return s_ if d == 0 else 7 - s_
                interleave([prep_gen(cd(0, 0), 0, 0), prep_gen(cd(0, 1), 1, 0)])
                for s in range(8):
                    gens = [scan_gen(cd(s, 0), 0, s % 2, s == 0), scan_gen(cd(s, 1), 1, s % 2, s == 0)]
                    if s + 1 < 8:
                        gens += [prep_gen(cd(s + 1, 0), 0, (s + 1) % 2), prep_gen(cd(s + 1, 1), 1, (s + 1) % 2)]
                    interleave(gens)

                gon = rowp[:, R_GON:R_GON + 128].unsqueeze(1).to_broadcast([128, 4, 128])
                for tb in range(8):
                    ACT(scr[:, 0:512], oacc[:, tb, :], AF.Square)
                    RSUM('dve', stat[:, 20:24], scr[:, 0:512].rearrange("p (a b) -> p a b", a=4))
                    rstd_from_ssq(stat[:, 24:28], stat[:, 20:24], 128.0)
                    TT('dve', ob32[:], oacc[:, tb, :].rearrange("p (a b) -> p a b", a=4),
                       stat[:, 24:28].unsqueeze(2).to_broadcast([128, 4, 128]), ALU.mult)
                    TT('dve', obb[:], ob32[:], gon, ALU.mult)
                    for j in range(4):
                        TR(psb[:, j * 128:(j + 1) * 128], obb[:, j, :], identb[:])
                    TT('dve', brT[:, 4:8, tb * 128:(tb + 1) * 128], psb[:, 0:512].rearrange("p (a b) -> p a b", a=4),
                       brT[:, 4:8, tb * 128:(tb + 1) * 128], ALU.mult)

        marks = []
        build_program.marks = marks
        build_program.prog = P
        import os
        KSTOP = int(os.environ.get("KSTOP", "99"))
        for l in range(2):
            if KSTOP <= 0:
                break
            marks.append(('L%d start' % l, len(P.ops['pe'])))
            do_mod_norm(l)
            marks.append(('L%d mod/norm done' % l, len(P.ops['pe'])))
            if KSTOP <= 1:
                break
            if STAGE_A:
                phase_a(l)
            else:
                I('dve', 'memset', ap=brT[:, 0:4, :], constant=0.0)
            marks.append(('L%d A done' % l, len(P.ops['pe'])))
            if STAGE_B:
                phase_b(l)
                nrot[0] = 5
            else:
                I('dve', 'memset', ap=brT[:, 4:8, :], constant=0.0)
            marks.append(('L%d B done' % l, len(P.ops['pe'])))
            if not STAGE_C:
                I('dve', 'memset', ap=brT[:, 8:12, :], constant=0.0)
            marks.append(('L%d C done' % l, len(P.ops['pe'])))
            phase_merge(l)
            marks.append(('L%d merge done' % l, len(P.ops['pe'])))

        yv = y_o.rearrange("(t p) d -> p t d", p=128)
        for tb in range(8):
            DMA('sp' if tb % 2 == 0 else 'act', yv[:, tb, :], x32[:, tb, :])
        P.final_wait('sp')
        P.emit()
    return nc


def _rope_tables():
    nf = 8
    inv = (10000.0 ** (-np.arange(nf, dtype=np.float32) / nf)).astype(np.float32)
    t = np.arange(1024)
    row = (t // 64).astype(np.float32)
    col = (t % 64).astype(np.float32)
    ar = row[:, None] * inv[None, :]
    ac = col[:, None] * inv[None, :]
    cosr, sinr, cosc, sinc = np.cos(ar), np.sin(ar), np.cos(ac), np.sin(ac)
    C = np.concatenate([cosr, cosr, cosc, cosc], axis=1).astype(np.float32)
    S = np.concatenate([-sinr, sinr, -sinc, sinc], axis=1).astype(np.float32)
    Cf = np.ones((1280, 32), np.float32)
    Sf = np.zeros((1280, 32), np.float32)
    Cf[256:] = C
    Sf[256:] = S
    to = lambda a: np.ascontiguousarray(a.reshape(10, 128, 32).transpose(1, 0, 2))
    return to(Cf), to(Sf)


_NC_CACHE = {}


def kernel(x_prompt, x_sample, cache_ckv, cache_krope, state_gdn, c, c_ctx, norm_g, w_mod, b_mod,
           w_in, q_a_norm, w_uq, kv_a_norm, w_ukv, q_norm, k_norm, conv_w, a_log, dt_bias,
           gdn_onorm, cm_ln_g, cm_ln_b, w_s, b_s, w_branch, w_o):
    f = lambda a: np.ascontiguousarray(np.asarray(a, dtype=np.float32))
    x_prompt, x_sample, cache_ckv, cache_krope, state_gdn, c, c_ctx = map(f, (x_prompt, x_sample, cache_ckv, cache_krope, state_gdn, c, c_ctx))
    norm_g, w_mod, b_mod, w_in, q_a_norm, w_uq, kv_a_norm, w_ukv, q_norm, k_norm = map(f, (norm_g, w_mod, b_mod, w_in, q_a_norm, w_uq, kv_a_norm, w_ukv, q_norm, k_norm))
    conv_w, a_log, dt_bias, gdn_onorm, cm_ln_g, cm_ln_b, w_s, b_s, w_branch, w_o = map(f, (conv_w, a_log, dt_bias, gdn_onorm, cm_ln_g, cm_ln_b, w_s, b_s, w_branch, w_o))
    L = 2
    rowp = np.zeros((L, NROW), np.float32)
    colp = np.zeros((128, L * NCOL), np.float32)
    for l in range(L):
        rowp[l, R_KVN:R_KVN + 256] = kv_a_norm[l]
        rowp[l, R_QN:R_QN + 96] = q_norm[l]
        rowp[l, R_KN:R_KN + 96] = k_norm[l]
        rowp[l, R_GON:R_GON + 128] = gdn_onorm[l]
        rowp[l, R_LNG:R_LNG + 512] = cm_ln_g[l]
        rowp[l, R_LNB:R_LNB + 512] = cm_ln_b[l]
        rowp[l, R_ALOG:R_ALOG + 8] = a_log[l].reshape(8)
        rowp[l, R_DTB:R_DTB + 8] = dt_bias[l].reshape(8)
        rowp[l, R_QAN:R_QAN + 384] = q_a_norm[l]
        b = l * NCOL
        colp[:, b + C_NG:b + C_NG + 8] = norm_g[l].reshape(8, 128).T
        colp[:, b + C_BMOD:b + C_BMOD + 24] = b_mod[l].reshape(24, 128).T
        colp[:, b + C_CONV:b + C_CONV + 60] = conv_w[l].reshape(5, 12, 128).transpose(2, 1, 0).reshape(128, 60)
        colp[:, b + C_BS:b + C_BS + 4] = b_s[l].T
    bgate = np.ascontiguousarray(b_mod[:, 2048:3072])
    w_sT = np.ascontiguousarray(w_s.transpose(0, 3, 1, 2))
    p_ = np.arange(128)[:, None]
    f_ = np.arange(128)[None, :]
    masks = np.stack([(f_ >= p_), (f_ <= p_), (f_ > p_), (f_ < p_)], axis=1).astype(np.float32)
    ident = np.eye(128, dtype=np.float32)
    ropec, ropes = _rope_tables()
    ropec_id = np.ones_like(ropec)
    ropes_id = np.zeros_like(ropes)
    BIG = -30000.0
    in_maps = []
    for r in range(8):
        m = dict(w_mod=w_mod, w_in=w_in, w_uq=w_uq, w_ukv=w_ukv, w_sT=w_sT, w_br=w_branch, w_o=w_o,
                 rowp=rowp, colp=colp, bgate=bgate, masks=masks, ident=ident)
        s0 = np.zeros((2, 4, 2, 4, 128, 128), np.float32)
        if r < 4:
            bb = r
            m["xin"] = x_sample[bb]
            m["ccol"] = np.ascontiguousarray(c[bb].reshape(8, 128).T)
            m["cck"] = cache_ckv[bb]
            m["ckr"] = cache_krope[bb]
            s0[:, 0, 0] = state_gdn[bb, :, 0]
            s0[:, 3, 1] = state_gdn[bb, :, 1]
            m["keep"] = np.ones((128, 1), np.float32)
            m["ropec"], m["ropes"] = ropec, ropes
            qs = np.zeros((4, 1024), np.float32)
            qs[0] = 1.0
            m["qseg"] = qs
            m["kpen"] = np.zeros((4, 1280), np.float32)
        else:
            j = r - 4
            m["xin"] = np.ascontiguousarray(x_prompt[4 * j:4 * j + 4].reshape(1024, 1024))
            m["ccol"] = np.ascontiguousarray(c_ctx.reshape(8, 128).T)
            m["cck"] = np.zeros((2, 256, 256), np.float32)
            m["ckr"] = np.zeros((2, 256, 32), np.float32)
            m["keep"] = np.zeros((128, 1), np.float32)
            m["ropec"], m["ropes"] = ropec_id, ropes_id
            qs = np.zeros((4, 1024), np.float32)
            kp = np.full((4, 1280), BIG, np.float32)
            for s in range(4):
                qs[s, s * 256:(s + 1) * 256] = 1.0
                kp[s, 256 + s * 256:256 + (s + 1) * 256] = 0.0
            m["qseg"] = qs
            m["kpen"] = kp
        m["s0"] = s0
        in_maps.append(m)
    if "nc" not in _NC_CACHE:
        _NC_CACHE["nc"] = build_program()
    nc = _NC_CACHE["nc"]
    import os
    ncore = int(os.environ.get("KCORES", "8"))
    if ncore < 8:
        in_maps = [in_maps[0], in_maps[4]][:ncore]
        res = run_bass_kernel_spmd(nc, in_maps, core_ids=list(range(ncore)))
        return tuple(res.results[0][k] for k in ("y", "nckv", "nkr", "nst"))
    res = run_bass_kernel_spmd(nc, in_maps, core_ids=list(range(8)))
    R = res.results
    y_sample = np.stack([R[r]["y"] for r in range(4)], axis=0).astype(np.float32)
    y_prompt = np.concatenate([R[r]["y"].reshape(4, 256, 1024) for r in range(4, 8)], axis=0).astype(np.float32)
    new_ckv = np.concatenate([R[r]["nckv"].reshape(2, 4, 256, 256).transpose(1, 0, 2, 3) for r in range(4, 8)], axis=0)
    new_kr = np.concatenate([R[r]["nkr"].reshape(2, 4, 256, 32).transpose(1, 0, 2, 3) for r in range(4, 8)], axis=0)
    new_st = np.concatenate([R[r]["nst"].transpose(1, 0, 2, 3, 4, 5) for r in range(4, 8)], axis=0)
    return (y_prompt, y_sample, np.ascontiguousarray(new_ckv, dtype=np.float32),
            np.ascontiguousarray(new_kr, dtype=np.float32), np.ascontiguousarray(new_st, dtype=np.float32))
```

```python
import numpy as np
import concourse.bass as bass
import concourse.mybir as mybir

F32 = mybir.dt.float32
BF16 = mybir.dt.bfloat16
ALU = mybir.AluOpType
AF = mybir.ActivationFunctionType
AX = mybir.AxisListType

_DT_SIZE = {F32: 4, BF16: 2}


def _dsize(dt):
    if dt in _DT_SIZE:
        return _DT_SIZE[dt]
    return mybir.dt.size(dt)


def ap_box(ap):
    t = ap.tensor
    name = t.name
    dims = ap.ap
    esz = _dsize(ap.dtype)
    off = int(ap.offset)
    shape = list(t.shape)
    space = str(ap.space)
    if 'DRAM' in space.upper() or 'HBM' in space.upper() or 'dram' in space.lower():
        lo = off
        hi = off
        for (st, cnt) in dims:
            if st >= 0:
                hi += st * (cnt - 1)
            else:
                lo += st * (cnt - 1)
        return (name, 0, 1, lo * esz, (hi + 1) * esz)
    tsz = _dsize(t.dtype)
    row = 1
    for s in shape[1:]:
        row *= s
    row_e = row * tsz // esz
    pstep, pcnt = dims[0]
    p0 = off // row_e
    fo = off % row_e
    if pstep == 0:
        np_ = 1
    else:
        np_ = pcnt * (pstep // row_e) if pstep >= row_e else 1
        np_ = (pcnt - 1) * (pstep // row_e) + 1
    lo = fo
    hi = fo
    for (st, cnt) in dims[1:]:
        if st >= 0:
            hi += st * (cnt - 1)
        else:
            lo += st * (cnt - 1)
    return (name, p0, p0 + np_, lo * esz, (hi + 1) * esz)


ENGS = ['pe', 'act', 'dve', 'pool', 'sp']
SAME_DIST = 10


class Prog:
    def __init__(self, nc, n_dma_sems=20):
        self.nc = nc
        self.ops = {e: [] for e in ENGS}
        self.clock = {e: {} for e in ENGS}
        self.snap = {}
        self.recs = {}
        self.n_dma = n_dma_sems
        self.dma_val = [0] * n_dma_sems
        self.dma_rr = 0
        self.dma_rr_sw = 0
        self.n_sw = 8
        self.waited = {e: set() for e in ENGS}
        self.out_events = []

    def _known(self, eng, ev):
        c = self.clock[eng]
        if ev[0] == 'e':
            return c.get(ev[1], -1) >= ev[2]
        return c.get(('d', ev[1]), 0) >= ev[2]

    def _learn(self, eng, ev):
        c = self.clock[eng]
        s = self.snap.get(ev)
        if s:
            for k, v in s.items():
                if c.get(k, -1) < v:
                    c[k] = v
        if ev[0] == 'e':
            if c.get(ev[1], -1) < ev[2]:
                c[ev[1]] = ev[2]
        else:
            k = ('d', ev[1])
            if c.get(k, 0) < ev[2]:
                c[k] = ev[2]

    def add(self, eng, fn, reads=(), writes=(), dma=False, extra_deps=()):
        idx = len(self.ops[eng])
        deps = set(extra_deps)

        def bx(a):
            b = ap_box(a)
            if b[0].startswith('ps'):
                return (b[0], 0, 128, 0, 1 << 20)
            return b
        rboxes = [bx(a) for a in reads]
        wboxes = [bx(a) for a in writes]
        for boxes, isw in ((rboxes, False), (wboxes, True)):
            for (name, p0, p1, b0, b1) in boxes:
                ps = name.startswith('ps')
                for r in self.recs.get(name, ()):
                    if r[0] < p1 and p0 < r[1] and r[2] < b1 and b0 < r[3]:
                        if isw or r[5] or (ps and r[6] != eng):
                            deps.add(r[4])
        if dma:
            if eng == 'pool':
                semi = self.dma_rr_sw
                self.dma_rr_sw = (self.dma_rr_sw + 1) % self.n_sw
            else:
                semi = self.n_sw + self.dma_rr
                self.dma_rr = (self.dma_rr + 1) % (self.n_dma - self.n_sw)
            prev = self.dma_val[semi]
            if prev > 0:
                deps.add(('d', semi, prev))
            self.dma_val[semi] = prev + 16
            ev = ('d', semi, prev + 16)
        else:
            ev = ('e', eng, idx)
        waits = []
        for d in sorted(deps, key=lambda d: -d[2]):
            if d[0] == 'e' and d[1] == eng:
                if eng == 'pe' and not dma:
                    continue
                if self._known(eng, d):
                    continue
                if (idx - d[2]) > SAME_DIST and not dma:
                    continue
                waits.append(d)
                self.waited[eng].add(d[2])
                self._learn(eng, d)
                continue
            if self._known(eng, d):
                continue
            waits.append(d)
            if d[0] == 'e':
                self.waited[d[1]].add(d[2])
            self._learn(eng, d)
        self.snap[ev] = dict(self.clock[eng])
        self.ops[eng].append(dict(fn=fn, waits=waits, dma=dma, ev=ev))
        acc_eng = ('dma:' + eng) if dma else eng
        for (name, p0, p1, b0, b1) in wboxes:
            lst = self.recs.setdefault(name, [])
            if name.startswith('ps'):
                lst[:] = []
            else:
                lst[:] = [r for r in lst if not (p0 <= r[0] and r[1] <= p1 and b0 <= r[2] and r[3] <= b1)]
            lst.append([p0, p1, b0, b1, ev, True, acc_eng])
        for (name, p0, p1, b0, b1) in rboxes:
            lst = self.recs.setdefault(name, [])
            if name.startswith('ps'):
                lst[:] = [r for r in lst if r[6] == eng and r[5]]
            elif not dma:
                lst[:] = [r for r in lst if not ((not r[5]) and r[4][0] == 'e' and r[4][1] == eng
                                                 and p0 <= r[0] and r[1] <= p1 and b0 <= r[2] and r[3] <= b1)]
            lst.append([p0, p1, b0, b1, ev, False, acc_eng])
        return ev

    def barrier(self):
        evs = []
        for e in ENGS:
            n = len(self.ops[e])
            if n:
                for i in range(n - 1, -1, -1):
                    if (not self.ops[e][i]['dma']) and self.ops[e][i]['fn'] is not None:
                        evs.append(('e', e, i))
                        break
        for i, v in enumerate(self.dma_val):
            if v > 0:
                evs.append(('d', i, v))
        for e in ENGS:
            self.add(e, None, extra_deps=[x for x in evs if not (x[0] == 'e' and x[1] == e)])
        self.recs = {}

    def final_wait(self, eng='sp'):
        evs = [('d', i, v) for i, v in enumerate(self.dma_val) if v > 0]
        self.add(eng, None, extra_deps=evs)

    def emit(self):
        nc = self.nc
        import contextlib
        with contextlib.ExitStack() as st:
            esem = {e: st.enter_context(nc.semaphore('s_' + e)) for e in ENGS}
            dsem = [st.enter_context(nc.semaphore('d%d' % i)) for i in range(self.n_dma)]
            rank = {}
            for e in ENGS:
                c = 0
                for i, op in enumerate(self.ops[e]):
                    if (not op['dma']) and i in self.waited[e]:
                        c += 1
                        rank[(e, i)] = c
            block = st.enter_context(nc.Block())

            def run(e, engobj):
                for i, op in enumerate(self.ops[e]):
                    for w in op['waits']:
                        if w[0] == 'e':
                            engobj.wait_ge(esem[w[1]], rank[(w[1], w[2])])
                        else:
                            engobj.wait_ge(dsem[w[1]], w[2])
                    if op['fn'] is None:
                        assert (e, i) not in rank
                        continue
                    ins = op['fn'](engobj)
                    if op['dma']:
                        ins.then_inc(dsem[op['ev'][1]], 16)
                    elif (e, i) in rank:
                        ins.then_inc(esem[e], 1)

            @block.tensor
            def _(eng):
                run('pe', eng)

            @block.scalar
            def _(eng):
                run('act', eng)

            @block.vector
            def _(eng):
                run('dve', eng)

            @block.gpsimd
            def _(eng):
                run('pool', eng)

            @block.sync
            def _(eng):
                run('sp', eng)


from concourse.bass_utils import run_bass_kernel_spmd
import contextlib, math

D = 1024
NROW = 2000
NCOL = 99
R_KVN, R_QN, R_KN, R_GON, R_LNG, R_LNB, R_ALOG, R_DTB, R_QAN = 0, 256, 352, 448, 576, 1088, 1600, 1608, 1616
C_NG, C_BMOD, C_CONV, C_BS = 0, 8, 32, 95
EPS = 1e-6
STAGE_A, STAGE_B, STAGE_C = 1, 1, 1
B2_STAGGER = 16
O_CQ, O_CKV, O_KR, O_ZA, O_QKV, O_GA, O_GB, O_ZB, O_CU, O_CV, O_ZC, O_GL = 0, 384, 640, 672, 1184, 2720, 2728, 2736, 3248, 3760, 4272, 4784


def isap(v):
    return isinstance(v, bass.AP)


def build_program(stage=99, dbg=False):
    nc = bass.Bass("TRN2", target_bir_lowering=False)

    def din(name, shape):
        return nc.dram_tensor(name, shape, F32, kind="ExternalInput").ap()

    def dout(name, shape):
        return nc.dram_tensor(name, shape, F32, kind="ExternalOutput").ap()

    xin = din("xin", [1024, 1024])
    ccol = din("ccol", [128, 8])
    cck = din("cck", [2, 256, 256])
    ckr = din("ckr", [2, 256, 32])
    s0d = din("s0", [2, 4, 2, 4, 128, 128])
    keepd = din("keep", [128, 1])
    ropec = din("ropec", [128, 10, 32])
    ropes = din("ropes", [128, 10, 32])
    qseg = din("qseg", [4, 1024])
    kpen = din("kpen", [4, 1280])
    rowpd = din("rowp", [2, NROW])
    colpd = din("colp", [128, 2 * NCOL])
    bgate = din("bgate", [2, 1024])
    w_mod = din("w_mod", [2, 1024, 3072])
    w_in = din("w_in", [2, 1024, 7856])
    w_uq = din("w_uq", [2, 384, 768])
    w_ukv = din("w_ukv", [2, 256, 1024])
    w_sT = din("w_sT", [2, 128, 4, 128])
    w_br = din("w_br", [2, 3, 512, 1024])
    w_o = din("w_o", [2, 1024, 1024])
    masksd = din("masks", [128, 4, 128])
    identd = din("ident", [128, 128])
    y_o = dout("y", [1024, 1024])
    nckv_o = dout("nckv", [2, 1024, 256])
    nkr_o = dout("nkr", [2, 1024, 32])
    nst_o = dout("nst", [2, 4, 2, 4, 128, 128])

    P = Prog(nc, n_dma_sems=24)

    def I(eng, meth, **kw):
        wr = [kw[k] for k in ('out', 'accum_out') if isap(kw.get(k))]
        rd = [v for k, v in kw.items() if k not in ('out', 'accum_out') and isap(v)]
        return P.add(eng, lambda e: getattr(e, meth)(**kw), rd, wr)

    def DMA(eng, out, in_):
        return P.add(eng, lambda e: e.dma_start(out=out, in_=in_), [in_], [out], dma=True)

    def MM(out, lhsT, rhs, start=True, stop=True):
        return I('pe', 'matmul', out=out, lhsT=lhsT, rhs=rhs, start=start, stop=stop)

    def TR(out, in_, ident):
        return I('pe', 'transpose', out=out, in_=in_, identity=ident)

    def ACT(out, in_, func, **kw):
        return I('act', 'activation', out=out, in_=in_, func=func, **kw)

    def TT(eng, out, in0, in1, op):
        return I(eng, 'tensor_tensor', out=out, in0=in0, in1=in1, op=op)

    def TS(eng, out, in0, s1, op0, s2=None, op1=None):
        if op1 is None:
            return I(eng, 'tensor_scalar', out=out, in0=in0, scalar1=s1, scalar2=None, op0=op0)
        return I(eng, 'tensor_scalar', out=out, in0=in0, scalar1=s1, scalar2=s2, op0=op0, op1=op1)

    def STT(eng, out, in0, scalar, in1, op0, op1):
        return I(eng, 'scalar_tensor_tensor', out=out, in0=in0, scalar=scalar, in1=in1, op0=op0, op1=op1)

    def CP(eng, out, in_):
        return I(eng, 'tensor_copy', out=out, in_=in_)

    def RSUM(eng, out, in_):
        return I(eng, 'tensor_reduce', out=out, in_=in_, axis=AX.X, op=ALU.add)

    def rstd_from_ssq(out, ssq, n):
        ACT(out, ssq, AF.Sqrt, bias=EPS, scale=1.0 / n)
        I('dve', 'reciprocal', out=out, in_=out)

    with contextlib.ExitStack() as st:
        ucnt = [0]
        pers_bytes = [0]
        arena_box = {}
        arena_top = [0]

        def sb(name, shape, dt=F32, stack=None):
            esz = 4 if dt == F32 else 2
            n = 1
            for d_ in shape[1:]:
                n *= d_
            if stack is None:
                pers_bytes[0] += (n * esz + 31) // 32 * 32
                return st.enter_context(nc.sbuf_tensor(name, shape, dt))
            if 'a' not in arena_box:
                ab = (212800 - pers_bytes[0] - 256) // 64 * 64
                arena_box['a'] = st.enter_context(nc.sbuf_tensor("arena", [128, ab // 2], BF16))
                arena_box['n'] = ab
            nbytes = n * esz
            nb_al = (nbytes + 31) // 32 * 32
            off = arena_top[0]
            assert off + nb_al <= arena_box['n'], ("arena overflow", name, off, nb_al, arena_box['n'])
            arena_top[0] = off + nb_al

            def release(o=off):
                arena_top[0] = o
            stack.callback(release)
            v = arena_box['a'][:, off // 2:off // 2 + nbytes // 2]
            if dt == F32:
                v = v.bitcast(F32)
            if len(shape) > 2:
                names = ["d%d" % i for i in range(len(shape) - 1)]
                pat = "p (%s) -> p %s" % (" ".join(names), " ".join(names))
                v = v.rearrange(pat, **{nm: int(sz) for nm, sz in zip(names, shape[1:])})
            return v

        psf = [st.enter_context(nc.psum_tensor("ps%d" % i, [128, 512], F32)) for i in range(7)]
        psb = st.enter_context(nc.psum_tensor("psb", [128, 1024], BF16))
        pcnt = [0]

        nrot = [5]

        def pnext():
            t = psf[pcnt[0] % nrot[0]]
            pcnt[0] += 1
            return t

        x32 = sb("x32", [128, 8, 1024])
        hT = sb("hT", [128, 8, 1024], BF16)
        brT = sb("brT", [128, 12, 1024], BF16)
        NSLOT = 4
        wring = [sb("wr%d" % i, [128, 4096], BF16) for i in range(NSLOT)]
        wcnt = [0]
        identf = sb("identf", [128, 128])
        identb = sb("identb", [128, 128], BF16)
        masks = sb("masksb", [128, 4, 128])
        onesf = sb("onesf", [128, 128])
        onesb = sb("onesb", [128, 128], BF16)
        rowp = sb("rowpt", [128, NROW])
        colp = sb("colpt", [128, 2 * NCOL])
        ccs = sb("ccs", [128, 8])
        sc32 = sb("sc32", [128, 8])
        scb = sb("scb", [128, 8], BF16)
        screp = sb("screp", [128, 8, 128], BF16)
        gate_bc = sb("gate_bc", [128, 1024])
        modcol = sb("modcol", [128, 16])
        gmod = sb("gmod", [128, 8])
        keep = sb("keept", [128, 1])
        rcos = sb("rcos", [128, 10, 32])
        rsin = sb("rsin", [128, 10, 32])
        stat = sb("stat", [128, 64])
        scr = sb("scr", [128, 1024])
        scr2 = sb("scr2", [128, 1024])

        def wslot():
            t = wring[wcnt[0] % NSLOT]
            wcnt[0] += 1
            return t

        pref = {}

        def load_w(src2d, kch, ncols, slot=None, key=None):
            if key is not None and key in pref:
                return pref.pop(key)
            t = wslot() if slot is None else wring[slot]
            v = t[:, 0:kch * ncols].rearrange("p (k c) -> p k c", k=kch)
            DMA('pool', v, src2d.rearrange("(k p) c -> p k c", p=128))
            return v

        def prefetch(key, src2d, kch, ncols, slot):
            pref[key] = load_w(src2d, kch, ncols, slot=slot)

        DMA('sp', identf[:], identd)
        DMA('sp', masks[:], masksd)
        DMA('sp', colp[:], colpd)
        DMA('sp', ccs[:], ccol)
        DMA('sp', keep[:], keepd)
        DMA('sp', rcos[:], ropec)
        DMA('sp', rsin[:], ropes)
        xv = xin.rearrange("(t p) d -> p t d", p=128)
        for tb in range(8):
            DMA('sp' if tb % 2 == 0 else 'act', x32[:, tb, :], xv[:, tb, :])
        I('dve', 'memset', ap=onesf[:], constant=1.0)
        I('dve', 'memset', ap=onesb[:], constant=1.0)
        CP('dve', identb[:], identf[:])
        ACT(sc32[:], ccs[:], AF.Silu)
        CP('dve', scb[:], sc32[:])
        CP('dve', screp[:], sc32[:].unsqueeze(2).to_broadcast([128, 8, 128]))


        def do_mod_norm(l):
            cb0 = l * NCOL
            ng_col = colp[:, cb0 + C_NG:cb0 + C_NG + 8]
            bmod_col = colp[:, cb0 + C_BMOD:cb0 + C_BMOD + 24]
            DMA('sp', rowp[:], rowpd[l:l + 1, :].to_broadcast([128, NROW]))
            DMA('sp', gate_bc[:], bgate[l:l + 1, :].to_broadcast([128, 1024]))
            import os
            KSUB = int(os.environ.get("KSUB", "99"))
            if KSUB <= 0:
                return
            pmod = pnext()
            for piece in range(4):
                wv = load_w(w_mod[l][:, piece * 512:(piece + 1) * 512], 8, 512, slot=(3 + piece) % 4, key=('mod', l, piece))
                for j in range(4):
                    col = piece * 4 + j
                    for kc in range(8):
                        MM(pmod[:, col:col + 1], wv[:, kc, j * 128:(j + 1) * 128], scb[:, kc:kc + 1],
                           start=(kc == 0), stop=(kc == 7))
            TT('dve', modcol[:], pmod[:, 0:16], bmod_col[:, 0:16], ALU.add)
            STT('dve', gmod[:], modcol[:, 8:16], 1.0, ng_col, ALU.add, ALU.mult)
            if KSUB <= 1:
                return
            for n in range(2):
                wv = load_w(w_mod[l][:, 2048 + n * 512:2048 + (n + 1) * 512], 8, 512, slot=(3 + 4 + n) % 4, key=('mod', l, 4 + n))
                pg = pnext()
                for kc in range(8):
                    MM(pg[:], screp[:, kc, :], wv[:, kc, :], start=(kc == 0), stop=(kc == 7))
                TT('dve', gate_bc[:, n * 512:(n + 1) * 512], pg[:], gate_bc[:, n * 512:(n + 1) * 512], ALU.add)
            if STAGE_A:
                prefetch(('za', l), w_in[l][:, O_ZA:O_ZA + 512], 8, 512, 1)
                prefetch(('wA0', l), w_in[l][:, 0:512], 8, 512, 2)
                prefetch(('wA1', l), w_in[l][:, 512:672], 8, 160, 3)
            with contextlib.ExitStack() as ns:
                sqb = [sb("n_sq", [128, 1024], F32, ns) for _ in range(2)]
                xsb = [sb("n_xs", [128, 1024], BF16, ns) for _ in range(3)]
                for tb in range(8):
                    sq_ = sqb[tb % 2]
                    xs_ = xsb[tb % 3]
                    ACT(sq_[:], x32[:, tb, :], AF.Square)
                    RSUM('dve', stat[:, tb:tb + 1], sq_[:])
                    rstd_from_ssq(stat[:, 8 + tb:9 + tb], stat[:, tb:tb + 1], 1024.0)
                    ACT(xs_[:], x32[:, tb, :], AF.Copy, scale=stat[:, 8 + tb:9 + tb])
                    pa, pb = pnext(), pnext()
                    pvs = (pa[:].bitcast(BF16), pb[:].bitcast(BF16))
                    for kc in range(8):
                        TR(pvs[kc // 4][:, (kc % 4) * 128:(kc % 4 + 1) * 128], xs_[:, kc * 128:(kc + 1) * 128], identb[:])
                    for kc in range(8):
                        src = pvs[kc // 4][:, (kc % 4) * 128:(kc % 4 + 1) * 128]
                        dst = hT[:, kc, tb * 128:(tb + 1) * 128]
                        if kc < 4:
                            TS('dve', dst, src, gmod[:, kc:kc + 1], ALU.mult, modcol[:, kc:kc + 1], ALU.add)
                        else:
                            ACT(dst, src, AF.Identity, scale=gmod[:, kc:kc + 1], bias=modcol[:, kc:kc + 1])

        def silu_zT(l, col0, br0, slot=None, key=None):
            wz = load_w(w_in[l][:, col0:col0 + 512], 8, 512, slot=slot, key=key)
            for j in range(4):
                for hf in range(2):
                    pz = pnext()
                    for kc in range(8):
                        MM(pz[:], wz[:, kc, j * 128:(j + 1) * 128], hT[:, kc, hf * 512:(hf + 1) * 512],
                           start=(kc == 0), stop=(kc == 7))
                    ACT(brT[:, br0 + j, hf * 512:(hf + 1) * 512], pz[:], AF.Silu)

        def gelu_from_psum(dst, ps, t2):
            ACT(dst, ps, AF.Gelu_apprx_tanh)

        def phase_c_gen(l, cs, slots):
            cb0 = l * NCOL
            bs_col = colp[:, cb0 + C_BS:cb0 + C_BS + 4]
            wsT = sb("wsT", [128, 4, 128], BF16, cs)
            gu = sb("c_gu", [128, 512], F32, cs)
            gv = sb("c_gv", [128, 512], F32, cs)
            ct2 = sb("c_t2", [128, 512], F32, cs)
            vn = sb("c_vn", [128, 512], BF16, cs)
            oc = sb("c_oc", [128, 512], BF16, cs)
            cst = sb("c_st", [128, 4], F32, cs)
            DMA('pool', wsT[:], w_sT[l])
            silu_zT(l, O_ZC, 8, slot=slots[0], key=('zc', l))
            yield
            wcu = load_w(w_in[l][:, O_CU:O_CU + 512], 8, 512, slot=slots[0])
            wcv = load_w(w_in[l][:, O_CV:O_CV + 512], 8, 512, slot=slots[1])
            lng = rowp[:, R_LNG:R_LNG + 512]
            lnb = rowp[:, R_LNB:R_LNB + 512]
            for tb in range(8):
                pcu = pnext()
                for kc in range(8):
                    MM(pcu[:], hT[:, kc, tb * 128:(tb + 1) * 128], wcu[:, kc, :], start=(kc == 0), stop=(kc == 7))
                gelu_from_psum(gu[:], pcu[:], ct2[:])
                yield
                pcv = pnext()
                for kc in range(8):
                    MM(pcv[:], hT[:, kc, tb * 128:(tb + 1) * 128], wcv[:, kc, :], start=(kc == 0), stop=(kc == 7))
                gelu_from_psum(gv[:], pcv[:], ct2[:])
                yield
                s_sum, s_mu, s_ss, s_rs = (cst[:, 0:1], cst[:, 1:2], cst[:, 2:3], cst[:, 3:4])
                RSUM('dve', s_sum, gv[:])
                TS('dve', s_mu, s_sum, -1.0 / 512.0, ALU.mult)
                ACT(gv[:], gv[:], AF.Identity, bias=s_mu)
                ACT(ct2[:], gv[:], AF.Square)
                RSUM('dve', s_ss, ct2[:])
                rstd_from_ssq(s_rs, s_ss, 512.0)
                STT('dve', gv[:], gv[:], s_rs, lng, ALU.mult, ALU.mult)
                TT('dve', vn[:], gv[:], lnb, ALU.add)
                yield
                psv = pnext()
                for g in range(4):
                    MM(psv[:, g * 128:(g + 1) * 128], wsT[:, g, :], vn[:, g * 128:(g + 1) * 128])
                for g in range(4):
                    STT('dve', oc[:, g * 128:(g + 1) * 128], psv[:, g * 128:(g + 1) * 128], bs_col[:, g:g + 1],
                        gu[:, g * 128:(g + 1) * 128], ALU.add, ALU.mult)
                yield
                for j in range(4):
                    TR(psb[:, j * 128:(j + 1) * 128], oc[:, j * 128:(j + 1) * 128], identb[:])
                TT('dve', brT[:, 8:12, tb * 128:(tb + 1) * 128],
                   psb[:, 0:512].rearrange("p (a b) -> p a b", a=4),
                   brT[:, 8:12, tb * 128:(tb + 1) * 128], ALU.mult)
                yield

        def interleave(gens):
            gens = list(gens)
            while gens:
                for g_ in list(gens):
                    try:
                        next(g_)
                    except StopIteration:
                        gens.remove(g_)

        def phase_merge(l):
            with contextlib.ExitStack() as cs:
                mT32 = sb("mT32", [128, 8, 1024], F32, cs)
                mT = sb("mT", [128, 8, 1024], BF16, cs)
                gsb = sb("gsb", [128, 2, 512], F32, cs)
                tmp = sb("mtmp", [128, 2, 512], F32, cs)
                pieces = []
                for n in range(3):
                    pieces.append((w_br[l, n], 4, 1024))
                    for hh in range(2):
                        pieces.append((w_in[l][:, O_GL + n * 1024 + hh * 512:O_GL + n * 1024 + (hh + 1) * 512], 8, 512))
                for hh in range(2):
                    pieces.append((w_o[l][:, hh * 512:(hh + 1) * 512], 8, 512))
                loaded = {}
                PF = 2

                def need(k):
                    for kk in range(len(loaded), min(k + PF, len(pieces))):
                        src, kch, ncol = pieces[kk]
                        loaded[kk] = load_w(src, kch, ncol, slot=kk % NSLOT)
                    return loaded[k]

                for n in range(3):
                    wbr = need(3 * n)
                    for oc_ in range(8):
                        wgp = need(3 * n + 1 + oc_ // 4)
                        for hf in range(2):
                            pyb = pnext()
                            pgl = pnext()
                            for kc in range(4):
                                MM(pyb[:], wbr[:, kc, oc_ * 128:(oc_ + 1) * 128], brT[:, n * 4 + kc, hf * 512:(hf + 1) * 512],
                                   start=(kc == 0), stop=(kc == 3))
                            for kc in range(8):
                                MM(pgl[:], wgp[:, kc, (oc_ % 4) * 128:(oc_ % 4 + 1) * 128],
                                   hT[:, kc, hf * 512:(hf + 1) * 512], start=(kc == 0), stop=(kc == 7))
                            ACT(gsb[:, hf, :], pgl[:], AF.Sigmoid)
                            msl = mT32[:, oc_, hf * 512:(hf + 1) * 512]
                            if n == 0:
                                TT('dve', msl, pyb[:], gsb[:, hf, :], ALU.mult)
                            else:
                                TT('dve', tmp[:, hf, :], pyb[:], gsb[:, hf, :], ALU.mult)
                                if n == 1:
                                    TT('dve', msl, msl, tmp[:, hf, :], ALU.add)
                                else:
                                    TT('dve', mT[:, oc_, hf * 512:(hf + 1) * 512], msl, tmp[:, hf, :], ALU.add)
                wo = [need(9), need(10)]
                if l + 1 < 2:
                    prefetch(('mod', l + 1, 0), w_mod[l + 1][:, 0:512], 8, 512, 3)
                    prefetch(('mod', l + 1, 1), w_mod[l + 1][:, 512:1024], 8, 512, 0)
                for tb in range(8):
                    for hh in range(2):
                        py = pnext()
                        for kc in range(8):
                            MM(py[:], mT[:, kc, tb * 128:(tb + 1) * 128], wo[hh][:, kc, :], start=(kc == 0), stop=(kc == 7))
                        TT('dve', tmp[:, hh, :], py[:], gate_bc[:, hh * 512:(hh + 1) * 512], ALU.mult)
                        TT('dve', x32[:, tb, hh * 512:(hh + 1) * 512], x32[:, tb, hh * 512:(hh + 1) * 512], tmp[:, hh, :], ALU.add)

        def rope(xv, blk, outv, rt, ru):
            Cb = rcos[:, blk, :].unsqueeze(1).to_broadcast([128, 8, 32])
            TT('dve', rt[:], xv, Cb, ALU.mult)
            x5 = xv.rearrange("p h (a t i) -> p h a t i", a=2, t=2)
            u5 = ru[:].rearrange("p h (a t i) -> p h a t i", a=2, t=2)
            S4 = rsin[:, blk, :].rearrange("p (a t i) -> p a t i", a=2, t=2)
            for t_ in range(2):
                Sb_ = S4[:, :, t_, :].unsqueeze(1).to_broadcast([128, 8, 2, 8])
                TT('dve', u5[:, :, :, t_, :], x5[:, :, :, 1 - t_, :], Sb_, ALU.mult)
            TT('dve', outv, rt[:], ru[:], ALU.add)

        def rope2d(xv, blk, outv, rt_, ru_):
            TT('dve', rt_, xv, rcos[:, blk, :], ALU.mult)
            x4 = xv.rearrange("p (a t i) -> p a t i", a=2, t=2)
            u4 = ru_.rearrange("p (a t i) -> p a t i", a=2, t=2)
            S4 = rsin[:, blk, :].rearrange("p (a t i) -> p a t i", a=2, t=2)
            for t_ in range(2):
                TT('dve', u4[:, :, t_, :], x4[:, :, 1 - t_, :], S4[:, :, t_, :], ALU.mult)
            TT('dve', outv, rt_, ru_, ALU.add)

        def phase_a(l):
            with contextlib.ExitStack() as cs:
                qT = sb("qT", [128, 8, 1024], BF16, cs)
                kT = sb("kT", [128, 8, 1280], BF16, cs)
                vx = sb("vx", [128, 10, 8, 65], BF16, cs)
                cT = [sb("cT%d" % i, [128, 5, 128], BF16, cs) for i in range(2)]
                cqn = [sb("cqn%d" % i, [128, 384], BF16, cs) for i in range(2)]
                ckv32 = [sb("ckv32_%d" % i, [128, 256], F32, cs) for i in range(2)]
                kr32 = [sb("kr32_%d" % i, [128, 32], F32, cs) for i in range(2)]
                ckvb = [sb("ckvb%d" % i, [128, 256], BF16, cs) for i in range(2)]
                cq32 = sb("cq32", [128, 384], F32, cs)
                qn32 = sb("qn32", [128, 8, 96], F32, cs)
                qf = sb("qf", [128, 8, 96], BF16, cs)
                rt = sb("rt", [128, 8, 32], F32, cs)
                ru = sb("ru", [128, 8, 32], F32, cs)
                kn32 = sb("kn32", [128, 8, 96], F32, cs)
                kf = sb("kf", [128, 8, 96], BF16, cs)
                krr = [sb("krr%d" % i, [128, 32], F32, cs) for i in range(2)]
                krg = sb("krg", [128, 32], F32, cs)
                krt = sb("krt", [128, 32], F32, cs)
                kru = sb("kru", [128, 32], F32, cs)
                kscr = sb("kscr", [128, 544], F32, cs)
                PT = [sb("PT%d" % i, [128, 512], BF16, cs) for i in range(3)]
                opair = sb("opair", [128, 4, 128], BF16, cs)
                rden = sb("rden", [128, 4], F32, cs)
                qanrow = rowp[:, R_QAN:R_QAN + 384]
                kvnrow = rowp[:, R_KVN:R_KVN + 256]
                qnrow = rowp[:, R_QN:R_QN + 96]
                knrow = rowp[:, R_KN:R_KN + 96]
                for h in range(8):
                    DMA('pool', qT[96:100, h, :], qseg)
                    DMA('pool', kT[96:100, h, :], kpen)
                I('dve', 'memset', ap=vx[:, :, :, 64:65], constant=1.0)
                wuq = load_w(w_uq[l], 3, 768, slot=0)
                silu_zT(l, O_ZA, 0, slot=1, key=('za', l))
                wA0 = load_w(w_in[l][:, 0:512], 8, 512, slot=2, key=('wA0', l))
                wA1 = load_w(w_in[l][:, 512:672], 8, 160, slot=3, key=('wA1', l))
                wukv = load_w(w_ukv[l], 2, 1024, slot=1)

                def kr_side(i, st_, k32):
                    ACT(kscr[:, 512:544], k32[:], AF.Square)
                    RSUM('dve', stat[:, 32 + st_:33 + st_], kscr[:, 512:544])
                    TT('dve', krg[:], k32[:], knrow[:, 64:96], ALU.mult)
                    rope2d(krg[:], i, krr[st_][:], krt[:], kru[:])

                def F_gen(i):
                    st_ = i % 2
                    c32, k32 = ckv32[st_], kr32[st_]
                    if i < 2:
                        DMA('sp', c32[:], cck[l, i * 128:(i + 1) * 128, :])
                        DMA('sp', k32[:], ckr[l, i * 128:(i + 1) * 128, :])
                        CP('dve', ckvb[st_][:], c32[:])
                        yield
                        for c_ in range(2):
                            TR(psb[:, c_ * 128:(c_ + 1) * 128], ckvb[st_][:, c_ * 128:(c_ + 1) * 128], identb[:])
                        CP('dve', cT[st_][:, 3:5, :], psb[:, 0:256].rearrange("p (a b) -> p a b", a=2))
                        yield
                        kr_side(i, st_, k32)
                        yield
                        return
                    tb = i - 2
                    pA0, pA1 = pnext(), pnext()
                    for kc in range(8):
                        MM(pA0[:], hT[:, kc, tb * 128:(tb + 1) * 128], wA0[:, kc, :], start=(kc == 0), stop=(kc == 7))
                    for kc in range(8):
                        MM(pA1[:, 0:160], hT[:, kc, tb * 128:(tb + 1) * 128], wA1[:, kc, :], start=(kc == 0), stop=(kc == 7))
                    I('act', 'copy', out=cq32[:], in_=pA0[:, 0:384])
                    I('act', 'copy', out=c32[:, 0:128], in_=pA0[:, 384:512])
                    I('act', 'copy', out=c32[:, 128:256], in_=pA1[:, 0:128])
                    CP('dve', k32[:], pA1[:, 128:160])
                    yield
                    ACT(scr2[:, 0:384], cq32[:], AF.Square)
                    RSUM('dve', stat[:, 20:21], scr2[:, 0:384])
                    rstd_from_ssq(stat[:, 21:22], stat[:, 20:21], 384.0)
                    STT('dve', cqn[st_][:], cq32[:], stat[:, 21:22], qanrow, ALU.mult, ALU.mult)
                    yield
                    ACT(scr2[:, 384:640], c32[:], AF.Square)
                    RSUM('dve', stat[:, 22:23], scr2[:, 384:640])
                    rstd_from_ssq(stat[:, 23:24], stat[:, 22:23], 256.0)
                    STT('dve', c32[:], c32[:], stat[:, 23:24], kvnrow, ALU.mult, ALU.mult)
                    CP('dve', ckvb[st_][:], c32[:])
                    DMA('sp', nckv_o[l, tb * 128:(tb + 1) * 128, :], c32[:])
                    DMA('sp', nkr_o[l, tb * 128:(tb + 1) * 128, :], k32[:])
                    yield
                    for c_ in range(3):
                        TR(psb[:, c_ * 128:(c_ + 1) * 128], cqn[st_][:, c_ * 128:(c_ + 1) * 128], identb[:])
                    for c_ in range(2):
                        TR(psb[:, (3 + c_) * 128:(4 + c_) * 128], ckvb[st_][:, c_ * 128:(c_ + 1) * 128], identb[:])
                    I('act', 'copy', out=cT[st_][:], in_=psb[:, 0:640].rearrange("p (a b) -> p a b", a=5))
                    yield
                    kr_side(i, st_, k32)
                    yield

                def Q_gen(i):
                    st_ = i % 2
                    tb = i - 2
                    pq0, pq1 = pnext(), pnext()
                    for c_ in range(3):
                        MM(pq0[:, 0:480], cT[st_][:, c_, :], wuq[:, c_, 0:480], start=(c_ == 0), stop=(c_ == 2))
                    for c_ in range(3):
                        MM(pq1[:, 0:288], cT[st_][:, c_, :], wuq[:, c_, 480:768], start=(c_ == 0), stop=(c_ == 2))
                    I('act', 'copy', out=qn32[:, 0:5, :], in_=pq0[:, 0:480].rearrange("p (h d) -> p h d", h=5))
                    I('act', 'copy', out=qn32[:, 5:8, :], in_=pq1[:, 0:288].rearrange("p (h d) -> p h d", h=3))
                    yield
                    ACT(scr[:, 0:768].rearrange("p (h d) -> p h d", h=8), qn32[:], AF.Square)
                    RSUM('dve', stat[:, 48:56], scr[:, 0:768].rearrange("p (h d) -> p h d", h=8))
                    TT('dve', qn32[:], qn32[:], qnrow.unsqueeze(1).to_broadcast([128, 8, 96]), ALU.mult)
                    ACT(stat[:, 56:64], stat[:, 48:56], AF.Sqrt, bias=EPS, scale=1.0 / 96.0)
                    yield
                    rope(qn32[:, :, 64:96], i, qn32[:, :, 64:96], rt, ru)
                    yield
                    I('dve', 'reciprocal', out=stat[:, 56:64], in_=stat[:, 56:64])
                    TT('dve', qf[:], qn32[:], stat[:, 56:64].unsqueeze(2).to_broadcast([128, 8, 96]), ALU.mult)
                    yield
                    for h in range(8):
                        TR(psb[0:96, h * 128:(h + 1) * 128], qf[:, h, :], identb[:])
                    I('act', 'copy', out=qT[0:96, :, tb * 128:(tb + 1) * 128], in_=psb[0:96, :].rearrange("p (h t) -> p h t", h=8))
                    yield

                def K_gen(i):
                    st_ = i % 2
                    kr = kr32[st_]
                    pk = [pnext(), pnext()]
                    for b_ in range(2):
                        for c_ in range(2):
                            MM(pk[b_][:], cT[st_][:, 3 + c_, :], wukv[:, c_, b_ * 512:(b_ + 1) * 512], start=(c_ == 0), stop=(c_ == 1))
                    for b_ in range(2):
                        pv = pk[b_][:].rearrange("p (h d) -> p h d", h=4)
                        I('act', 'copy', out=vx[:, i, b_ * 4:(b_ + 1) * 4, 0:64], in_=pv[:, :, 64:128])
                        CP('dve', kn32[:, b_ * 4:(b_ + 1) * 4, 0:64], pv[:, :, 0:64])
                    yield
                    ACT(kscr[:, 0:512].rearrange("p (h d) -> p h d", h=8), kn32[:, :, 0:64], AF.Square)
                    RSUM('dve', stat[:, 24:32], kscr[:, 0:512].rearrange("p (h d) -> p h d", h=8))
                    TS('dve', stat[:, 24:32], stat[:, 24:32], stat[:, 32 + st_:33 + st_], ALU.add)
                    ACT(stat[:, 40:48], stat[:, 24:32], AF.Sqrt, bias=EPS, scale=1.0 / 96.0)
                    TT('dve', kn32[:, :, 0:64], kn32[:, :, 0:64], knrow[:, 0:64].unsqueeze(1).to_broadcast([128, 8, 64]), ALU.mult)
                    yield
                    I('dve', 'reciprocal', out=stat[:, 40:48], in_=stat[:, 40:48])
                    TT('dve', kf[:, :, 0:64], kn32[:, :, 0:64], stat[:, 40:48].unsqueeze(2).to_broadcast([128, 8, 64]), ALU.mult)
                    TT('dve', kf[:, :, 64:96], krr[st_][:].unsqueeze(1).to_broadcast([128, 8, 32]),
                       stat[:, 40:48].unsqueeze(2).to_broadcast([128, 8, 32]), ALU.mult)
                    yield
                    for h in range(8):
                        TR(psb[0:96, h * 128:(h + 1) * 128], kf[:, h, :], identb[:])
                    CP('dve', kT[0:96, :, i * 128:(i + 1) * 128], psb[0:96, :].rearrange("p (h t) -> p h t", h=8))
                    yield

                interleave([F_gen(0)])
                for i in range(10):
                    gens = [K_gen(i)]
                    if i >= 2:
                        gens.append(Q_gen(i))
                    if i + 1 <= 9:
                        gens.append(F_gen(i + 1))
                    interleave(gens)
                if STAGE_B:
                    prefetch(('zb', l), w_in[l][:, O_ZB:O_ZB + 512], 8, 512, 0)
                    prefetch(('qkv0', l), w_in[l][:, O_QKV:O_QKV + 512], 8, 512, 1)
                    if STAGE_C:
                        prefetch(('zc', l), w_in[l][:, O_ZC:O_ZC + 512], 8, 512, 2)
                sc_ = 1.0 / math.sqrt(96.0)
                items = [(hp, hf, hl, kb) for hp in range(4) for hf in range(2) for hl in range(2) for kb in range(10)]
                LOOK = 2
                pS_q = []

                def emit_S(idx):
                    hp, hf, hl, kb = items[idx]
                    h = hp * 2 + hl
                    pS = pnext()
                    MM(pS[:], kT[0:100, h, kb * 128:(kb + 1) * 128], qT[0:100, h, hf * 512:(hf + 1) * 512])
                    pS_q.append(pS)

                for i0 in range(min(LOOK, len(items))):
                    emit_S(i0)
                for idx, (hp, hf, hl, kb) in enumerate(items):
                    h = hp * 2 + hl
                    po = psf[5 + hl]
                    pov = po[:, 0:260].rearrange("p (j e) -> p j e", j=4)
                    if kb == 0:
                        I('dve', 'memset', ap=po[:, 0:260], constant=0.0)
                    pS = pS_q.pop(0)
                    if idx + LOOK < len(items):
                        emit_S(idx + LOOK)
                    pt = PT[idx % 3]
                    ACT(pt[:], pS[:], AF.Exp, scale=sc_)
                    for j in range(4):
                        I('pe', 'matmul', out=pov[:, j, :], lhsT=pt[:, j * 128:(j + 1) * 128], rhs=vx[:, kb, h, :],
                          start=False, stop=(kb == 9), skip_group_check=True)
                    if kb == 9:
                        I('dve', 'reciprocal', out=rden[:], in_=pov[:, :, 64])
                        TT('dve', opair[:, :, hl * 64:(hl + 1) * 64], pov[:, :, 0:64],
                           rden[:].unsqueeze(2).to_broadcast([128, 4, 64]), ALU.mult)
                        if hl == 1:
                            for j in range(4):
                                TR(psb[:, j * 128:(j + 1) * 128], opair[:, j, :], identb[:])
                            TT('dve', brT[:, hp, hf * 512:(hf + 1) * 512], psb[:, 0:512], brT[:, hp, hf * 512:(hf + 1) * 512], ALU.mult)

        def phase_b(l):
            cb0 = l * NCOL
            nrot[0] = 7
            with contextlib.ExitStack() as cs:
                qTb = sb("qTb", [128, 4, 1024], BF16, cs)
                kTb = sb("kTb", [128, 4, 1024], BF16, cs)
                vtok = sb("vtok", [128, 8, 512], BF16, cs)
                oacc = sb("oacc", [128, 8, 512], F32, cs)
                gab = sb("gab", [128, 8, 16], F32, cs)
                gg = sb("gg", [128, 8, 8], F32, cs)
                bet = sb("bet", [128, 8, 8], F32, cs)
                gt1 = scr[:, 0:64].rearrange("p (a b) -> p a b", a=8)
                gt2 = scr[:, 64:128].rearrange("p (a b) -> p a b", a=8)
                ea = scr[:, 128:136]
                Gb = scr[:, 0:1024].rearrange("p (a b) -> p a b", a=8)
                DTi = scr2[:, 0:512].rearrange("p (a b) -> p a b", a=4)
                tmpf = scr2[:, 512:1024].rearrange("p (a b) -> p a b", a=4)
                acc = scr2[:, 0:1024].rearrange("p (s t) -> p s t", s=4)

                def v4(ps):
                    return ps[:].rearrange("p (a b) -> p a b", a=4)

                b1 = contextlib.ExitStack()
                xps = [sb("xp%d" % i, [128, 4, 260], BF16, b1) for i in range(2)]
                dgs = [sb("dg%d" % i, [128, 5, 128], BF16, b1) for i in range(2)]
                accs = [scr2, sb("acc1", [128, 1024], F32, b1)]
                sqs = [scr, sb("sq1", [128, 1032], F32, b1)]
                vtmp = sb("vtmp", [128, 1024], BF16, b1)
                I('dve', 'memset', ap=oacc[:], constant=0.0)
                for xp_ in xps:
                    I('dve', 'memset', ap=xp_[:], constant=0.0)

                def b1_gen():
                    silu_zT(l, O_ZB, 4, slot=0, key=('zb', l))
                    yield
                    for pi in range(3):
                        wq = load_w(w_in[l][:, O_QKV + pi * 512:O_QKV + (pi + 1) * 512], 8, 512, slot=(pi + 1) % 2, key=('qkv%d' % pi, l))
                        for j in range(4):
                            ch = pi * 4 + j
                            par = ch % 2
                            xp = xps[par]
                            dg = dgs[par]
                            accf = accs[par][:, 0:1024]
                            sq = sqs[par][:, 0:1024]
                            for hf in range(2):
                                pz = pnext()
                                for kc in range(8):
                                    MM(pz[:], wq[:, kc, j * 128:(j + 1) * 128], hT[:, kc, hf * 512:(hf + 1) * 512],
                                       start=(kc == 0), stop=(kc == 7))
                                I('act', 'copy', out=xp[:, 2 * hf:2 * hf + 2, 2:258], in_=pz[:].rearrange("p (s t) -> p s t", s=2))
                            yield
                            TS('dve', xp[:, 1:4, 0:2], xp[:, 0:3, 256:258], keep[:, 0:1], ALU.mult)
                            TS('dve', xp[:, 0:3, 258:260], xp[:, 1:4, 2:4], keep[:, 0:1], ALU.mult)
                            cw = colp[:, cb0 + C_CONV + ch * 5:cb0 + C_CONV + ch * 5 + 5]
                            TT('dve', dg[:], identf[:].unsqueeze(1).to_broadcast([128, 5, 128]),
                               cw.unsqueeze(2).to_broadcast([128, 5, 128]), ALU.mult)
                            yield
                            for h2 in range(2):
                                pc = pnext()
                                for sg in range(2):
                                    for jj in range(5):
                                        MM(pc[:, sg * 256:(sg + 1) * 256], dg[:, jj, :], xp[:, h2 * 2 + sg, jj:jj + 256],
                                           start=(jj == 0), stop=(jj == 4))
                                if pi == 2:
                                    ACT(vtmp[:, h2 * 512:(h2 + 1) * 512], pc[:], AF.Silu)
                                else:
                                    ACT(accf[:, h2 * 512:(h2 + 1) * 512], pc[:], AF.Silu)
                            yield
                            if pi == 2:
                                for tb in range(8):
                                    TR(psb[:, tb * 128:(tb + 1) * 128], vtmp[:, tb * 128:(tb + 1) * 128], identb[:])
                                CP('dve', vtok[:, :, j * 128:(j + 1) * 128], psb[:].rearrange("p (t d) -> p t d", t=8))
                                yield
                            else:
                                ACT(sq, accf, AF.Square)
                                dstT = qTb if pi == 0 else kTb
                                for hf in range(2):
                                    pn = pnext()
                                    MM(pn[:], onesf[:], sq[:, hf * 512:(hf + 1) * 512])
                                    ACT(sq[:, hf * 512:(hf + 1) * 512], pn[:], AF.Sqrt, bias=EPS, scale=1.0)
                                yield
                                I('dve', 'reciprocal', out=sq, in_=sq)
                                STT('dve', dstT[:, j, :], accf, (128.0 ** -0.5) if pi == 0 else 1.0, sq, ALU.mult, ALU.mult)
                                yield

                if STAGE_C:
                    interleave([b1_gen(), phase_c_gen(l, b1, (2, 3))])
                else:
                    interleave([b1_gen()])
                b1.close()
                S32 = [sb("S32_%d" % i, [128, 4, 128], F32, cs) for i in range(2)]
                Sbb = [sb("Sbb%d" % i, [128, 4, 128], BF16, cs) for i in range(2)]

                def f4(ap2d):
                    return ap2d.rearrange("p (a b) -> p a b", a=4)
                twoI = sb("twoI", [128, 128], F32, cs)
                TS('dve', twoI[:], identf[:], 2.0, ALU.mult)
                sets = []
                s0_ = dict(
                    gst=sb("gst0", [128, 16], F32, cs), gsm=[sb("gsm0_%d" % i, [128, 20], F32, cs) for i in range(3)],
                    Gb=f4(scr2[:, 512:1024]), DTi=f4(scr2[:, 0:512]), tmpf=f4(scr2[:, 512:1024]),
                    ktok=[sb("ktok0", [128, 4, 128], BF16, cs)[:], f4(scr[:, 512:768].bitcast(BF16))],
                    Mb=[sb("Mb0_%d" % i, [128, 4, 128], F32, cs)[:] for i in range(2)],
                    Nb=[sb("Nb0_%d" % i, [128, 4, 128], F32, cs)[:] for i in range(2)],
                    Pb=sb("Pb0", [128, 4, 128], F32, cs)[:], Rb=sb("Rb0", [128, 4, 128], BF16, cs)[:],
                    kgb=sb("kgb0", [128, 4, 128], BF16, cs)[:], vnb=sb("vnb0", [128, 4, 128], BF16, cs)[:],
                    tmp2=sb("tmp20", [128, 4, 128], F32, cs)[:],
                    attT=[sb("attT0_%d" % i, [128, 4, 128], BF16, cs)[:] for i in range(2)] + [f4(scr[:, 768:1024].bitcast(BF16))],
                    kpb=[sb("kpb0_%d" % i, [128, 4, 128], BF16, cs)[:] for i in range(2)],
                    u32=[sb("u32_0_%d" % i, [128, 4, 128], F32, cs)[:] for i in range(2)],
                    wTb=[sb("wTb0_%d" % i, [128, 4, 128], BF16, cs)[:] for i in range(2)])
                sets.append(s0_)
                wpos = [0, 0]

                def carve(nbytes, f32):
                    if wpos[1] + nbytes > 8192:
                        wpos[0] += 1
                        wpos[1] = 0
                    t = wring[wpos[0]]
                    e0 = wpos[1] // 2
                    v = t[:, e0:e0 + nbytes // 2]
                    wpos[1] += nbytes
                    if f32:
                        v = v.bitcast(F32)
                    return f4(v)
                s1_ = dict(gst=sb("gst1", [128, 16], F32, cs), gsm=[sb("gsm1_%d" % i, [128, 20], F32, cs) for i in range(3)])
                for nm in ("DTi", "tmpf", "Pb", "tmp2"):
                    s1_[nm] = carve(2048, True)
                u32a = carve(2048, True)
                s1_["Gb"] = s1_["tmpf"]
                s1_["Mb"] = [carve(2048, True) for _ in range(2)]
                s1_["Nb"] = [carve(2048, True) for _ in range(2)]
                s1_["ktok"] = [carve(1024, False) for _ in range(2)]
                for nm in ("Rb", "kgb", "vnb"):
                    s1_[nm] = carve(1024, False)
                s1_["attT"] = [carve(1024, False) for _ in range(3)]
                for nm in ("kpb", "wTb"):
                    s1_[nm] = [carve(1024, False) for _ in range(2)]
                s1_["u32"] = [u32a, carve(2048, True)]
                sets.append(s1_)
                ob32 = s0_["tmp2"]
                obb = s0_["kgb"]

                wgab = load_w(w_in[l][:, O_GA:O_GA + 16], 8, 16)
                pgab = pnext()
                for tb in range(8):
                    for kc in range(8):
                        MM(pgab[:, tb * 16:(tb + 1) * 16], hT[:, kc, tb * 128:(tb + 1) * 128], wgab[:, kc, :],
                           start=(kc == 0), stop=(kc == 7))
                CP('dve', gab[:], pgab[:, 0:128].rearrange("p (t c) -> p t c", t=8))
                dtb = rowp[:, R_DTB:R_DTB + 8].unsqueeze(1).to_broadcast([128, 8, 8])
                TT('dve', gt1[:], gab[:, :, 0:8], dtb, ALU.add)
                TS('dve', gt2[:], gt1[:], -1.0, ALU.mult)
                TT('dve', gt2[:], gt2[:], gt1[:], ALU.max)
                ACT(gt2[:], gt2[:], AF.Exp, scale=-1.0)
                ACT(gt2[:], gt2[:], AF.Ln, bias=1.0, scale=1.0)
                TS('dve', gt1[:], gt1[:], 0.0, ALU.max)
                TT('dve', gt1[:], gt1[:], gt2[:], ALU.add)
                ACT(ea[:], rowp[:, R_ALOG:R_ALOG + 8], AF.Exp)
                TT('dve', gg[:], gt1[:], ea[:].unsqueeze(1).to_broadcast([128, 8, 8]), ALU.mult)
                TS('dve', gg[:], gg[:], -1.0, ALU.mult)
                ACT(bet[:], gab[:, :, 8:16], AF.Sigmoid)

                def roles(T, t):
                    return (T["Pb"], T["Nb"][0]) if t % 2 == 0 else (T["Mb"][1], T["Mb"][0])

                def pre_gen(c, d, t):
                    T = sets[d]
                    gst, gsm, Gb, DTi, tmpf, ktok = T["gst"], T["gsm"][t % 3], T["Gb"], T["DTi"], T["tmpf"], T["ktok"][t % 2]
                    Xt, Bt = roles(T, t)
                    attT = T["attT"][t % 3]
                    cs_ = slice(c * 128, (c + 1) * 128)
                    mi, ms = (0, 2) if d == 0 else (1, 3)
                    dsl = slice(d * 4, d * 4 + 4)
                    pgc = pnext()
                    MM(pgc[:, 0:4], masks[:, mi, :], gg[:, c, dsl])
                    MM(pgc[:, 8:12], onesf[:], gg[:, c, dsl])
                    I('act', 'copy', out=gst[:, 0:16], in_=pgc[:, 0:16])
                    gcol = gst[:, 0:4]
                    glb = gst[:, 8:12]
                    egc, ekl, egl, nbeta, t4 = gsm[:, 0:4], gsm[:, 4:8], gsm[:, 8:12], gsm[:, 12:16], gsm[:, 16:20]
                    ACT(egc, gcol, AF.Exp)
                    TT('dve', t4, glb, gcol, ALU.subtract)
                    ACT(ekl, t4, AF.Exp)
                    ACT(egl, glb, AF.Exp)
                    TS('dve', nbeta, bet[:, c, dsl], -1.0, ALU.mult)
                    CP('dve', Gb, gg[:, c, dsl].unsqueeze(2).to_broadcast([128, 4, 128]))
                    yield
                    for h in range(4):
                        TR(psb[:, h * 128:(h + 1) * 128], kTb[:, h, cs_], identb[:])
                    I('act', 'copy', out=ktok, in_=psb[:, 0:512].rearrange("p (a b) -> p a b", a=4))
                    yield
                    pB = pnext()
                    for h in range(4):
                        MM(pB[:, h * 128:(h + 1) * 128], Gb[:, h, :], masks[:, mi, :])
                    for h in range(4):
                        ACT(DTi[:, h, :], pB[:, h * 128:(h + 1) * 128], AF.Relu, scale=-1.0, bias=gcol[:, h:h + 1])
                    ACT(DTi, DTi, AF.Exp, scale=-1.0)
                    TT('dve', DTi, DTi, masks[:, mi, :].unsqueeze(1).to_broadcast([128, 4, 128]), ALU.mult)
                    yield
                    pG = pnext()
                    for h in range(4):
                        MM(pG[:, h * 128:(h + 1) * 128], kTb[:, h, cs_], kTb[:, h, cs_])
                    TT('dve', tmpf, v4(pG), DTi, ALU.mult)
                    TT('dve', tmpf, tmpf, nbeta.unsqueeze(2).to_broadcast([128, 4, 128]), ALU.mult)
                    TT('dve', Bt, tmpf, masks[:, ms, :].unsqueeze(1).to_broadcast([128, 4, 128]), ALU.mult)
                    yield
                    pQK = pnext()
                    for h in range(4):
                        MM(pQK[:, h * 128:(h + 1) * 128], kTb[:, h, cs_], qTb[:, h, cs_])
                    TT('dve', attT, v4(pQK), DTi, ALU.mult)
                    TT('dve', Xt, Bt, identf[:].unsqueeze(1).to_broadcast([128, 4, 128]), ALU.add)
                    yield
                    pT0 = pnext()
                    for h in range(4):
                        TR(pT0[:, h * 128:(h + 1) * 128], Bt[:, h, :], identf[:])
                    idb4 = identf[:].unsqueeze(1).to_broadcast([128, 4, 128])
                    TT('dve', Bt, idb4, v4(pT0), ALU.subtract)
                    yield
                def ns_gen(c, d, t):
                    T = sets[d]
                    gsm, ktok, Rb, kgb = T["gsm"][t % 3], T["ktok"][t % 2], T["Rb"], T["kgb"]
                    Pb, NbB = roles(T, t)
                    NbG = T["Nb"][1]
                    kpb, u32, wTb = T["kpb"][t % 2], T["u32"][t % 2], T["wTb"][t % 2]
                    egc, ekl = gsm[:, 0:4], gsm[:, 4:8]
                    id2 = twoI[:].unsqueeze(1).to_broadcast([128, 4, 128])
                    for k in range(6):
                        for hh in range(2):
                            pE = pnext()
                            for h in (2 * hh, 2 * hh + 1):
                                MM(pE[:, (h % 2) * 128:(h % 2 + 1) * 128], Pb[:, h, :], NbB[:, h, :])
                            TT('dve', NbG[:, 2 * hh:2 * hh + 2, :], twoI[:].unsqueeze(1).to_broadcast([128, 2, 128]),
                               pE[:, 0:256].rearrange("p (a b) -> p a b", a=2), ALU.subtract)
                        yield
                        for hh in range(2):
                            pX = pnext()
                            for h in (2 * hh, 2 * hh + 1):
                                MM(pX[:, (h % 2) * 128:(h % 2 + 1) * 128], NbG[:, h, :], Pb[:, h, :])
                            I('act', 'copy', out=Pb[:, 2 * hh:2 * hh + 2, :], in_=pX[:, 0:256].rearrange("p (a b) -> p a b", a=2))
                        yield
                    I('act', 'copy', out=Rb, in_=Pb)
                    R = Rb
                    TT('dve', kgb, ktok, egc.unsqueeze(2).to_broadcast([128, 4, 128]), ALU.mult)
                    TT('dve', kpb, ktok, ekl.unsqueeze(2).to_broadcast([128, 4, 128]), ALU.mult)
                    yield
                    pU = pnext()
                    for h in range(4):
                        MM(pU[:, h * 128:(h + 1) * 128], R[:, h, :], vtok[:, c, h * 128:(h + 1) * 128])
                    I('act', 'copy', out=u32, in_=v4(pU))
                    yield
                    pW = pnext()
                    for h in range(4):
                        MM(pW[:, h * 128:(h + 1) * 128], kgb[:, h, :], R[:, h, :])
                    I('act', 'copy', out=wTb, in_=v4(pW))
                    yield
                def scan_gen(c, d, t, first):
                    T = sets[d]
                    gsm, vnb, tmp2 = T["gsm"][t % 3], T["vnb"], T["tmp2"]
                    attT, kpb, u32, wTb = T["attT"][t % 3], T["kpb"][t % 2], T["u32"][t % 2], T["wTb"][t % 2]
                    egc, egl = gsm[:, 0:4], gsm[:, 8:12]
                    cs_ = slice(c * 128, (c + 1) * 128)
                    dsl = slice(d * 4, d * 4 + 4)
                    S, Sb_ = S32[d], Sbb[d]
                    seg_start = (c % 2 == 0) if d == 0 else (c % 2 == 1)
                    if seg_start:
                        DMA('sp', tmp2, s0d[l, c // 2, d].rearrange("h k v -> k h v"))
                        if first:
                            CP('dve', S[:], tmp2)
                        else:
                            STT('dve', S[:], S[:], keep[:, 0:1], tmp2, ALU.mult, ALU.add)
                        CP('dve', Sb_[:], S[:])
                        yield
                    pWS = pnext()
                    for h in range(4):
                        MM(pWS[:, h * 128:(h + 1) * 128], wTb[:, h, :], Sb_[:, h, :])
                    TT('dve', tmp2, u32, v4(pWS), ALU.subtract)
                    TT('dve', vnb, tmp2, bet[:, c, dsl].unsqueeze(2).to_broadcast([128, 4, 128]), ALU.mult)
                    yield
                    pQS = pnext()
                    for h in range(4):
                        MM(pQS[:, h * 128:(h + 1) * 128], qTb[:, h, cs_], Sb_[:, h, :])
                    TT('dve', tmp2, v4(pQS), egc.unsqueeze(2).to_broadcast([128, 4, 128]), ALU.mult)
                    yield
                    pAV = pnext()
                    for h in range(4):
                        MM(pAV[:, h * 128:(h + 1) * 128], attT[:, h, :], vnb[:, h, :])
                    TT('dve', tmp2, tmp2, v4(pAV), ALU.add)
                    ov = oacc[:, c, :].rearrange("p (a b) -> p a b", a=4)
                    TT('pool', ov, ov, tmp2, ALU.add)
                    yield
                    pDS = pnext()
                    for h in range(4):
                        MM(pDS[:, h * 128:(h + 1) * 128], kpb[:, h, :], vnb[:, h, :])
                    TT('dve', S[:], S[:], egl.unsqueeze(2).to_broadcast([128, 4, 128]), ALU.mult)
                    TT('dve', S[:], S[:], v4(pDS), ALU.add)
                    CP('dve', Sb_[:], S[:])
                    seg_end = (c % 2 == 1) if d == 0 else (c % 2 == 0)
                    if seg_end:
                        DMA('sp', nst_o[l, c // 2, d].rearrange("h k v -> k h v"), S[:])
                    yield

                def cd(s_, d):
                    return s_ if d == 0 else 7 - s_
                def rr(gens):
                    gens = list(gens)
                    while gens:
                        for g_ in list(gens):
                            try:
                                next(g_)
                                yield
                            except StopIteration:
                                gens.remove(g_)

                def dir_gen(d):
                    yield from pre_gen(cd(0, d), d, 0)
                    yield from rr([ns_gen(cd(0, d), d, 0), pre_gen(cd(1, d), d, 1)])
                    for t in range(8):
                        g3 = [scan_gen(cd(t, d), d, t, t == 0)]
                        if t + 1 < 8:
                            g3.append(ns_gen(cd(t + 1, d), d, t + 1))
                        if t + 2 < 8:
                            g3.append(pre_gen(cd(t + 2, d), d, t + 2))
                        yield from rr(g3)

                interleave([dir_gen(0), dir_gen(1)])

                gon = rowp[:, R_GON:R_GON + 128].unsqueeze(1).to_broadcast([128, 4, 128])
                for tb in range(8):
                    ACT(scr[:, 0:512], oacc[:, tb, :], AF.Square)
                    RSUM('dve', stat[:, 20:24], scr[:, 0:512].rearrange("p (a b) -> p a b", a=4))
                    rstd_from_ssq(stat[:, 24:28], stat[:, 20:24], 128.0)
                    TT('dve', ob32[:], oacc[:, tb, :].rearrange("p (a b) -> p a b", a=4),
                       stat[:, 24:28].unsqueeze(2).to_broadcast([128, 4, 128]), ALU.mult)
                    TT('dve', obb[:], ob32[:], gon, ALU.mult)
                    for j in range(4):
                        TR(psb[:, j * 128:(j + 1) * 128], obb[:, j, :], identb[:])
                    TT('dve', brT[:, 4:8, tb * 128:(tb + 1) * 128], psb[:, 0:512].rearrange("p (a b) -> p a b", a=4),
                       brT[:, 4:8, tb * 128:(tb + 1) * 128], ALU.mult)

        marks = []
        build_program.marks = marks
        build_program.prog = P
        import os
        KSTOP = int(os.environ.get("KSTOP", "99"))
        for l in range(2):
            if KSTOP <= 0:
                break
            marks.append(('L%d start' % l, len(P.ops['pe'])))
            do_mod_norm(l)
            marks.append(('L%d mod/norm done' % l, len(P.ops['pe'])))
            if KSTOP <= 1:
                break
            if STAGE_A:
                phase_a(l)
            else:
                I('dve', 'memset', ap=brT[:, 0:4, :], constant=0.0)
            marks.append(('L%d A done' % l, len(P.ops['pe'])))
            if STAGE_B:
                phase_b(l)
                nrot[0] = 5
            else:
                I('dve', 'memset', ap=brT[:, 4:8, :], constant=0.0)
            marks.append(('L%d B done' % l, len(P.ops['pe'])))
            if not STAGE_C:
                I('dve', 'memset', ap=brT[:, 8:12, :], constant=0.0)
            marks.append(('L%d C done' % l, len(P.ops['pe'])))
            phase_merge(l)
            marks.append(('L%d merge done' % l, len(P.ops['pe'])))

        yv = y_o.rearrange("(t p) d -> p t d", p=128)
        for tb in range(8):
            DMA('sp' if tb % 2 == 0 else 'act', yv[:, tb, :], x32[:, tb, :])
        P.final_wait('sp')
        P.emit()
    return nc


def _rope_tables():
    nf = 8
    inv = (10000.0 ** (-np.arange(nf, dtype=np.float32) / nf)).astype(np.float32)
    t = np.arange(1024)
    row = (t // 64).astype(np.float32)
    col = (t % 64).astype(np.float32)
    ar = row[:, None] * inv[None, :]
    ac = col[:, None] * inv[None, :]
    cosr, sinr, cosc, sinc = np.cos(ar), np.sin(ar), np.cos(ac), np.sin(ac)
    C = np.concatenate([cosr, cosr, cosc, cosc], axis=1).astype(np.float32)
    S = np.concatenate([-sinr, sinr, -sinc, sinc], axis=1).astype(np.float32)
    Cf = np.ones((1280, 32), np.float32)
    Sf = np.zeros((1280, 32), np.float32)
    Cf[256:] = C
    Sf[256:] = S
    to = lambda a: np.ascontiguousarray(a.reshape(10, 128, 32).transpose(1, 0, 2))
    return to(Cf), to(Sf)


_NC_CACHE = {}


def kernel(x_prompt, x_sample, cache_ckv, cache_krope, state_gdn, c, c_ctx, norm_g, w_mod, b_mod,
           w_in, q_a_norm, w_uq, kv_a_norm, w_ukv, q_norm, k_norm, conv_w, a_log, dt_bias,
           gdn_onorm, cm_ln_g, cm_ln_b, w_s, b_s, w_branch, w_o):
    f = lambda a: np.ascontiguousarray(np.asarray(a, dtype=np.float32))
    x_prompt, x_sample, cache_ckv, cache_krope, state_gdn, c, c_ctx = map(f, (x_prompt, x_sample, cache_ckv, cache_krope, state_gdn, c, c_ctx))
    norm_g, w_mod, b_mod, w_in, q_a_norm, w_uq, kv_a_norm, w_ukv, q_norm, k_norm = map(f, (norm_g, w_mod, b_mod, w_in, q_a_norm, w_uq, kv_a_norm, w_ukv, q_norm, k_norm))
    conv_w, a_log, dt_bias, gdn_onorm, cm_ln_g, cm_ln_b, w_s, b_s, w_branch, w_o = map(f, (conv_w, a_log, dt_bias, gdn_onorm, cm_ln_g, cm_ln_b, w_s, b_s, w_branch, w_o))
    L = 2
    rowp = np.zeros((L, NROW), np.float32)
    colp = np.zeros((128, L * NCOL), np.float32)
    for l in range(L):
        rowp[l, R_KVN:R_KVN + 256] = kv_a_norm[l]
        rowp[l, R_QN:R_QN + 96] = q_norm[l]
        rowp[l, R_KN:R_KN + 96] = k_norm[l]
        rowp[l, R_GON:R_GON + 128] = gdn_onorm[l]
        rowp[l, R_LNG:R_LNG + 512] = cm_ln_g[l]
        rowp[l, R_LNB:R_LNB + 512] = cm_ln_b[l]
        rowp[l, R_ALOG:R_ALOG + 8] = a_log[l].reshape(8)
        rowp[l, R_DTB:R_DTB + 8] = dt_bias[l].reshape(8)
        rowp[l, R_QAN:R_QAN + 384] = q_a_norm[l]
        b = l * NCOL
        colp[:, b + C_NG:b + C_NG + 8] = norm_g[l].reshape(8, 128).T
        colp[:, b + C_BMOD:b + C_BMOD + 24] = b_mod[l].reshape(24, 128).T
        colp[:, b + C_CONV:b + C_CONV + 60] = conv_w[l].reshape(5, 12, 128).transpose(2, 1, 0).reshape(128, 60)
        colp[:, b + C_BS:b + C_BS + 4] = b_s[l].T
    bgate = np.ascontiguousarray(b_mod[:, 2048:3072])
    w_sT = np.ascontiguousarray(w_s.transpose(0, 3, 1, 2))
    p_ = np.arange(128)[:, None]
    f_ = np.arange(128)[None, :]
    masks = np.stack([(f_ >= p_), (f_ <= p_), (f_ > p_), (f_ < p_)], axis=1).astype(np.float32)
    ident = np.eye(128, dtype=np.float32)
    ropec, ropes = _rope_tables()
    ropec_id = np.ones_like(ropec)
    ropes_id = np.zeros_like(ropes)
    BIG = -30000.0
    in_maps = []
    for r in range(8):
        m = dict(w_mod=w_mod, w_in=w_in, w_uq=w_uq, w_ukv=w_ukv, w_sT=w_sT, w_br=w_branch, w_o=w_o,
                 rowp=rowp, colp=colp, bgate=bgate, masks=masks, ident=ident)
        s0 = np.zeros((2, 4, 2, 4, 128, 128), np.float32)
        if r < 4:
            bb = r
            m["xin"] = x_sample[bb]
            m["ccol"] = np.ascontiguousarray(c[bb].reshape(8, 128).T)
            m["cck"] = cache_ckv[bb]
            m["ckr"] = cache_krope[bb]
            s0[:, 0, 0] = state_gdn[bb, :, 0]
            s0[:, 3, 1] = state_gdn[bb, :, 1]
            m["keep"] = np.ones((128, 1), np.float32)
            m["ropec"], m["ropes"] = ropec, ropes
            qs = np.zeros((4, 1024), np.float32)
            qs[0] = 1.0
            m["qseg"] = qs
            m["kpen"] = np.zeros((4, 1280), np.float32)
        else:
            j = r - 4
            m["xin"] = np.ascontiguousarray(x_prompt[4 * j:4 * j + 4].reshape(1024, 1024))
            m["ccol"] = np.ascontiguousarray(c_ctx.reshape(8, 128).T)
            m["cck"] = np.zeros((2, 256, 256), np.float32)
            m["ckr"] = np.zeros((2, 256, 32), np.float32)
            m["keep"] = np.zeros((128, 1), np.float32)
            m["ropec"], m["ropes"] = ropec_id, ropes_id
            qs = np.zeros((4, 1024), np.float32)
            kp = np.full((4, 1280), BIG, np.float32)
            for s in range(4):
                qs[s, s * 256:(s + 1) * 256] = 1.0
                kp[s, 256 + s * 256:256 + (s + 1) * 256] = 0.0
            m["qseg"] = qs
            m["kpen"] = kp
        m["s0"] = s0
        in_maps.append(m)
    if "nc" not in _NC_CACHE:
        _NC_CACHE["nc"] = build_program()
    nc = _NC_CACHE["nc"]
    import os
    ncore = int(os.environ.get("KCORES", "8"))
    if ncore < 8:
        in_maps = [in_maps[0], in_maps[4]][:ncore]
        res = run_bass_kernel_spmd(nc, in_maps, core_ids=list(range(ncore)))
        return tuple(res.results[0][k] for k in ("y", "nckv", "nkr", "nst"))
    res = run_bass_kernel_spmd(nc, in_maps, core_ids=list(range(8)))
    R = res.results
    y_sample = np.stack([R[r]["y"] for r in range(4)], axis=0).astype(np.float32)
    y_prompt = np.concatenate([R[r]["y"].reshape(4, 256, 1024) for r in range(4, 8)], axis=0).astype(np.float32)
    new_ckv = np.concatenate([R[r]["nckv"].reshape(2, 4, 256, 256).transpose(1, 0, 2, 3) for r in range(4, 8)], axis=0)
    new_kr = np.concatenate([R[r]["nkr"].reshape(2, 4, 256, 32).transpose(1, 0, 2, 3) for r in range(4, 8)], axis=0)
    new_st = np.concatenate([R[r]["nst"].transpose(1, 0, 2, 3, 4, 5) for r in range(4, 8)], axis=0)
    return (y_prompt, y_sample, np.ascontiguousarray(new_ckv, dtype=np.float32),
            np.ascontiguousarray(new_kr, dtype=np.float32), np.ascontiguousarray(new_st, dtype=np.float32))
```
